# Optimizing a Trainium2 kernel written in Bass

```python
import math
import jax, jax.numpy as jnp
from jax import lax
import numpy as np


D_MODEL = 1024
BATCH = 8
SEQ = 4096
DEPTH = 4

GRID_W = 64
CTX_LEN = 256
HEAD_DIM = D_MODEL // 16
MLSTM_HEADS = 4
MLSTM_W = MLSTM_HEADS * HEAD_DIM
DIFF_HEADS = 4
DIFF_DV = 2 * HEAD_DIM
DIFF_W = DIFF_HEADS * DIFF_DV
GQA_HEADS = 4
GQA_KV_HEADS = 2
GQA_W = GQA_HEADS * HEAD_DIM
D_MIX = MLSTM_W + DIFF_W + GQA_W
MLSTM_CONV_W = 5
MLSTM_CHUNK = 64
Q_BLOCK = 128
ROPE_THETA = 10000.0
D_FF = (8 * D_MODEL + 3 * 256 - 1) // (3 * 256) * 256
ALPHA = (2 * DEPTH) ** 0.25
BETA = (8 * DEPTH) ** -0.25
LN_EPS = 1e-5
SPLIT_SIZES = (MLSTM_W, MLSTM_W, MLSTM_W, MLSTM_W, 4 * MLSTM_HEADS,
               2 * DIFF_HEADS * HEAD_DIM, 2 * DIFF_HEADS * HEAD_DIM, DIFF_W,
               GQA_W, GQA_KV_HEADS * HEAD_DIM, GQA_KV_HEADS * HEAD_DIM)
IN_COLS = sum(SPLIT_SIZES)

kernel_name = 'hybrid_mlstm_diffattn_gqa_dit_block'


def _layer_norm(x, g, b):
    xf = x.astype(jnp.float32)
    mu = xf.mean(-1, keepdims=True)
    var = jnp.square(xf - mu).mean(-1, keepdims=True)
    return ((xf - mu) * lax.rsqrt(var + LN_EPS) * g + b).astype(x.dtype)


def _rms_norm(x, g):
    xf = x.astype(jnp.float32)
    return (xf * lax.rsqrt(jnp.mean(xf * xf, -1, keepdims=True) + LN_EPS) * g).astype(x.dtype)


def _rope_tables(rows):
    row = jnp.repeat(jnp.arange(rows, dtype=jnp.float32), GRID_W)
    col = jnp.tile(jnp.arange(GRID_W, dtype=jnp.float32), rows)
    n_freq = HEAD_DIM // 4
    inv = ROPE_THETA ** (-jnp.arange(n_freq, dtype=jnp.float32) / n_freq)
    ar = row[:, None] * inv
    ac = col[:, None] * inv
    ang = jnp.concatenate([ar, ar, ac, ac], axis=-1)
    return jnp.cos(ang), jnp.sin(ang)


def _rot_half(u):
    u1, u2 = jnp.split(u, 2, axis=-1)
    return jnp.concatenate([-u2, u1], axis=-1)


def _apply_rope(x, cos, sin):
    xf = x.astype(jnp.float32)
    xr, xc = jnp.split(xf, 2, axis=-1)
    rot = jnp.concatenate([_rot_half(xr), _rot_half(xc)], axis=-1)
    return (xf * cos + rot * sin).astype(x.dtype)


def _heads(a, n_heads):
    b, s, _ = a.shape
    return a.reshape(b, s, n_heads, -1).transpose(0, 2, 1, 3)


def _merge_heads(a):
    b, h, s, d = a.shape
    return a.transpose(0, 2, 1, 3).reshape(b, s, h * d)


def _split_cols(p):
    out, start = [], 0
    for size in SPLIT_SIZES:
        out.append(p[..., start:start + size])
        start += size
    return out


def _dw_conv(u, w, b):
    k, ch = w.shape
    y = lax.conv_general_dilated(u, w[:, None, :], window_strides=(1,),
                                 padding=((k // 2, k // 2),),
                                 dimension_numbers=('NWC', 'WIO', 'NWC'),
                                 feature_group_count=ch)
    return y + b


def _to_blocks(a):
    *lead, s, d = a.shape
    return jnp.moveaxis(a.reshape(*lead, s // Q_BLOCK, Q_BLOCK, d), -3, 0)


def _from_blocks(a):
    a = jnp.moveaxis(a, 0, -3)
    *lead, nb, bq, d = a.shape
    return a.reshape(*lead, nb * bq, d)


def _diff_attention(q1, q2, k1, k2, v, lam):
    scale = HEAD_DIM ** -0.5

    def block(qs):
        qa, qb = qs
        p1 = jax.nn.softmax((jnp.einsum('bhqd,bhkd->bhqk', qa, k1) * scale).astype(jnp.float32), axis=-1)
        p2 = jax.nn.softmax((jnp.einsum('bhqd,bhkd->bhqk', qb, k2) * scale).astype(jnp.float32), axis=-1)
        return jnp.einsum('bhqk,bhkv->bhqv', (p1 - lam * p2).astype(v.dtype), v)

    return _from_blocks(lax.map(block, (_to_blocks(q1), _to_blocks(q2))))


def _gqa_attention(q, k, v):
    b, hq, s, d = q.shape
    qg = q.reshape(b, GQA_KV_HEADS, hq // GQA_KV_HEADS, s, d)
    scale = d ** -0.5

    def block(qb):
        p = jax.nn.softmax((jnp.einsum('bhgqd,bhkd->bhgqk', qb, k) * scale).astype(jnp.float32), axis=-1)
        return jnp.einsum('bhgqk,bhkd->bhgqd', p.astype(v.dtype), v)

    return _from_blocks(lax.map(block, _to_blocks(qg))).reshape(b, hq, s, d)


def _mlstm_scan(q, k, v, i_pre, f_pre, state):
    b, h, s, d = q.shape
    nc = s // MLSTM_CHUNK

    def chunks(a):
        return jnp.moveaxis(a.reshape(b, h, nc, MLSTM_CHUNK, *a.shape[3:]), 2, 0)

    xs = (chunks(q.astype(jnp.float32) * d ** -0.5), chunks(k.astype(jnp.float32)),
          chunks(v.astype(jnp.float32)), chunks(i_pre), chunks(jax.nn.log_sigmoid(f_pre)))
    lower = jnp.tril(jnp.ones((MLSTM_CHUNK, MLSTM_CHUNK), dtype=bool))

    def step(carry, inp):
        c_mat, n_vec, m = carry
        qc, kc, vc, ic, lf = inp
        bcum = jnp.cumsum(lf, axis=-1)
        logw = jnp.where(lower, bcum[..., :, None] - bcum[..., None, :] + ic[..., None, :], -jnp.inf)
        inter = bcum + m[..., None]
        m_t = jnp.maximum(logw.max(-1), inter)
        w = jnp.exp(logw - m_t[..., None])
        w_prev = jnp.exp(inter - m_t)
        sw = jnp.einsum('bhtd,bhsd->bhts', qc, kc) * w
        num = jnp.einsum('bhts,bhsv->bhtv', sw, vc) + w_prev[..., None] * jnp.einsum('bhtd,bhdv->bhtv', qc, c_mat)
        den = sw.sum(-1) + w_prev * jnp.einsum('bhtd,bhd->bht', qc, n_vec)
        h_out = num / jnp.maximum(jnp.abs(den), jnp.exp(-m_t))[..., None]
        b_last = bcum[..., -1]
        logu = b_last[..., None] - bcum + ic
        m_new = jnp.maximum(b_last + m, logu.max(-1))
        decay = jnp.exp(b_last + m - m_new)
        u = jnp.exp(logu - m_new[..., None])
        c_new = decay[..., None, None] * c_mat + jnp.einsum('bhs,bhsd,bhsv->bhdv', u, kc, vc)
        n_new = decay[..., None] * n_vec + jnp.einsum('bhs,bhsd->bhd', u, kc)
        return (c_new, n_new, m_new), h_out

    state, hs = lax.scan(step, state, xs)
    return jnp.moveaxis(hs, 0, 2).reshape(b, h, s, d), state


def _mlstm_prep(mq, mk, mv, mg, conv_w, conv_b, gate_b):
    qk = jax.nn.silu(_dw_conv(jnp.concatenate([mq, mk], axis=-1), conv_w, conv_b))
    q = _heads(qk[..., :MLSTM_W], MLSTM_HEADS)
    k = _heads(qk[..., MLSTM_W:], MLSTM_HEADS)
    v = _heads(mv, MLSTM_HEADS)
    g = (mg + gate_b).astype(jnp.float32).transpose(0, 2, 1)
    i_f, i_b, f_f, f_b = jnp.split(g, 4, axis=1)
    return q, k, v, (i_f, f_f), (i_b, f_b)


def _mlstm_bidirectional(lat, ctx):
    ql, kl, vl, fwd_l, bwd_l = lat
    qc, kc, vc, fwd_c, bwd_c = ctx
    b = ql.shape[0]
    zero = (jnp.zeros((b, MLSTM_HEADS, HEAD_DIM, HEAD_DIM), jnp.float32),
            jnp.zeros((b, MLSTM_HEADS, HEAD_DIM), jnp.float32),
            jnp.zeros((b, MLSTM_HEADS), jnp.float32))
    hcf, st_f = _mlstm_scan(qc, kc, vc, fwd_c[0], fwd_c[1], zero)
    hlf, _ = _mlstm_scan(ql, kl, vl, fwd_l[0], fwd_l[1], st_f)
    rev = lambda a: jnp.flip(a, axis=2)
    hcb, st_b = _mlstm_scan(rev(qc), rev(kc), rev(vc), rev(bwd_c[0]), rev(bwd_c[1]), zero)
    hlb, _ = _mlstm_scan(rev(ql), rev(kl), rev(vl), rev(bwd_l[0]), rev(bwd_l[1]), st_b)
    return hlf + rev(hlb), hcf + rev(hcb)


def _mlstm_out(h, o, g):
    mu = h.mean(-1, keepdims=True)
    var = jnp.square(h - mu).mean(-1, keepdims=True)
    hn = (h - mu) * lax.rsqrt(var + LN_EPS)
    return (_merge_heads(hn) * g * jax.nn.sigmoid(o.astype(jnp.float32))).astype(o.dtype)


def _diff_split(a):
    b, s, _ = a.shape
    a = a.reshape(b, s, DIFF_HEADS, 2, HEAD_DIM).transpose(0, 2, 3, 1, 4)
    return a[:, :, 0], a[:, :, 1]


def _diff_out(o, g, lam_init):
    return _merge_heads(_rms_norm(o, g) * (1.0 - lam_init))


def _hybrid_mixer(p_lat, p_ctx, conv_w, conv_b, gate_b, mnorm_g, lam_vecs, lam_init,
                  dnorm_g, qn_g, kn_g, cos, sin, ctx_out):
    mq, mk, mv, mo, mg, dq, dk, dv, gq, gk, gv = _split_cols(p_lat)
    mqc, mkc, mvc, moc, mgc, dqc, dkc, dvc, gqc, gkc, gvc = _split_cols(p_ctx)
    rope = lambda a: _apply_rope(a, cos, sin)
    h_lat, h_ctx = _mlstm_bidirectional(_mlstm_prep(mq, mk, mv, mg, conv_w, conv_b, gate_b),
                                        _mlstm_prep(mqc, mkc, mvc, mgc, conv_w, conv_b, gate_b))
    a_lat = _mlstm_out(h_lat, mo, mnorm_g)
    lv = lam_vecs.astype(jnp.float32)
    lam = jnp.exp(jnp.sum(lv[0] * lv[1])) - jnp.exp(jnp.sum(lv[2] * lv[3])) + lam_init
    q1, q2 = _diff_split(dq)
    k1, k2 = _diff_split(dk)
    k1c, k2c = _diff_split(dkc)
    vd, vdc = _heads(dv, DIFF_HEADS), _heads(dvc, DIFF_HEADS)
    k1_all = jnp.concatenate([k1c, rope(k1)], axis=2)
    k2_all = jnp.concatenate([k2c, rope(k2)], axis=2)
    vd_all = jnp.concatenate([vdc, vd], axis=2)
    b_lat = _diff_out(_diff_attention(rope(q1), rope(q2), k1_all, k2_all, vd_all, lam), dnorm_g, lam_init)
    qg = rope(_rms_norm(_heads(gq, GQA_HEADS), qn_g))
    kg = rope(_rms_norm(_heads(gk, GQA_KV_HEADS), kn_g))
    kgc = _rms_norm(_heads(gkc, GQA_KV_HEADS), kn_g)
    vg, vgc = _heads(gv, GQA_KV_HEADS), _heads(gvc, GQA_KV_HEADS)
    c_lat = _merge_heads(_gqa_attention(qg, jnp.concatenate([kgc, kg], axis=2),
                                        jnp.concatenate([vgc, vg], axis=2)))
    y_lat = jnp.concatenate([a_lat, b_lat, c_lat], axis=-1)
    if not ctx_out:
        return y_lat, None
    q1c, q2c = _diff_split(dqc)
    qgc = _rms_norm(_heads(gqc, GQA_HEADS), qn_g)
    a_ctx = _mlstm_out(h_ctx, moc, mnorm_g)
    b_ctx = _diff_out(_diff_attention(q1c, q2c, k1c, k2c, vdc, lam), dnorm_g, lam_init)
    cm_ctx = _merge_heads(_gqa_attention(qgc, kgc, vgc))
    return y_lat, jnp.concatenate([a_ctx, b_ctx, cm_ctx], axis=-1)


def _swiglu(u, w_in, w_out):
    gate, up = jnp.split(u @ w_in, 2, axis=-1)
    return (jax.nn.silu(gate) * up) @ w_out


def setup_inputs(seed: int = 0) -> dict:
    key = jax.random.key(seed)
    ks = jax.random.split(key, 24)
    nrm = lambda k, shape, s: s * jax.random.normal(k, shape, jnp.float32)
    gate_i = nrm(ks[9], (DEPTH, 2 * MLSTM_HEADS), 0.1)
    gate_f = jnp.tile(jnp.linspace(3.0, 6.0, MLSTM_HEADS, dtype=jnp.float32), 2) + nrm(ks[10], (DEPTH, 2 * MLSTM_HEADS), 0.1)
    return {
        'x': nrm(ks[0], (BATCH, SEQ, D_MODEL), 1.0),
        'c': nrm(ks[1], (BATCH, D_MODEL), 1.0),
        'ctx': nrm(ks[2], (BATCH, CTX_LEN, D_MODEL), 1.0),
        'c_ctx': nrm(ks[3], (D_MODEL,), 1.0),
        'w_ada': nrm(ks[4], (DEPTH, D_MODEL, 6 * D_MODEL), 0.5 * D_MODEL ** -0.5),
        'b_ada': nrm(ks[5], (DEPTH, 6 * D_MODEL), 0.01),
        'w_in': nrm(ks[6], (DEPTH, D_MODEL, IN_COLS), D_MODEL ** -0.5),
        'mlstm_conv_w': nrm(ks[7], (DEPTH, MLSTM_CONV_W, 2 * MLSTM_W), MLSTM_CONV_W ** -0.5),
        'mlstm_conv_b': nrm(ks[8], (DEPTH, 2 * MLSTM_W), 0.01),
        'mlstm_gate_b': jnp.concatenate([gate_i, gate_f], axis=-1),
        'mlstm_norm_g': 1.0 + nrm(ks[11], (DEPTH, MLSTM_W), 0.1),
        'diff_lambda': nrm(ks[12], (DEPTH, 4, HEAD_DIM), 0.1),
        'diff_norm_g': 1.0 + nrm(ks[13], (DEPTH, DIFF_DV), 0.1),
        'gqa_q_norm_g': 1.0 + nrm(ks[14], (DEPTH, HEAD_DIM), 0.1),
        'gqa_k_norm_g': 1.0 + nrm(ks[15], (DEPTH, HEAD_DIM), 0.1),
        'w_out': nrm(ks[16], (DEPTH, D_MIX, D_MODEL), BETA * D_MIX ** -0.5),
        'ln1_g': 1.0 + nrm(ks[17], (DEPTH, D_MODEL), 0.1),
        'ln1_b': nrm(ks[18], (DEPTH, D_MODEL), 0.01),
        'w_ffn_in': nrm(ks[19], (DEPTH, D_MODEL, 2 * D_FF), D_MODEL ** -0.5),
        'w_ffn_out': nrm(ks[20], (DEPTH, D_FF, D_MODEL), BETA * D_FF ** -0.5),
        'ln2_g': 1.0 + nrm(ks[21], (DEPTH, D_MODEL), 0.1),
        'ln2_b': nrm(ks[22], (DEPTH, D_MODEL), 0.01),
    }


def reference(x, c, ctx, c_ctx, w_ada, b_ada, w_in, mlstm_conv_w, mlstm_conv_b, mlstm_gate_b,
              mlstm_norm_g, diff_lambda, diff_norm_g, gqa_q_norm_g, gqa_k_norm_g, w_out,
              ln1_g, ln1_b, w_ffn_in, w_ffn_out, ln2_g, ln2_b):
    rows = x.shape[1] // GRID_W
    cos, sin = _rope_tables(rows)
    xc = ctx
    for l in range(DEPTH):
        last = l == DEPTH - 1
        lam_init = 0.8 - 0.6 * math.exp(-0.3 * l)
        mod = jax.nn.silu(c) @ w_ada[l] + b_ada[l]
        sh1, sc1, g1, sh2, sc2, g2 = jnp.split(mod[:, None, :], 6, axis=-1)
        modc = jax.nn.silu(c_ctx) @ w_ada[l] + b_ada[l]
        sh1c, sc1c, g1c, sh2c, sc2c, g2c = jnp.split(modc, 6)
        y, yc = _hybrid_mixer((x * (1 + sc1) + sh1) @ w_in[l], (xc * (1 + sc1c) + sh1c) @ w_in[l],
                              mlstm_conv_w[l], mlstm_conv_b[l], mlstm_gate_b[l], mlstm_norm_g[l],
                              diff_lambda[l], lam_init, diff_norm_g[l], gqa_q_norm_g[l], gqa_k_norm_g[l],
                              cos, sin, not last)
        x = _layer_norm(ALPHA * x + g1 * (y @ w_out[l]), ln1_g[l], ln1_b[l])
        x = _layer_norm(ALPHA * x + g2 * _swiglu(x * (1 + sc2) + sh2, w_ffn_in[l], w_ffn_out[l]), ln2_g[l], ln2_b[l])
        if not last:
            xc = _layer_norm(ALPHA * xc + g1c * (yc @ w_out[l]), ln1_g[l], ln1_b[l])
            xc = _layer_norm(ALPHA * xc + g2c * _swiglu(xc * (1 + sc2c) + sh2c, w_ffn_in[l], w_ffn_out[l]), ln2_g[l], ln2_b[l])
    return x
```

```python
import math
from contextlib import ExitStack
import numpy as np
import ml_dtypes
import concourse.bass as bass
import concourse.mybir as mybir
from concourse.bass_utils import run_bass_kernel_spmd

F32 = mybir.dt.float32
BF16 = mybir.dt.bfloat16
AF = mybir.ActivationFunctionType
ALU = mybir.AluOpType
AX = mybir.AxisListType

D = 1024
HD = 64
CTX = 256
DEPTH_FULL = 4
D_FF = 2816
IN_COLS = 3088
LN_EPS = 1e-5
C_MQ, C_MK, C_MV, C_MO, C_MG, C_DQ, C_DK, C_DV, C_GQ, C_GK, C_GV = (
    0, 256, 512, 768, 1024, 1040, 1552, 2064, 2576, 2832, 2960)

STORES_ON_POOL = True
SAME_SYNC = {'pe': False, 'act': True, 'dve': True, 'pool': True, 'sp': False}


class Buf:
    __slots__ = ('name', 'w', 'r', 'ap')

    def __init__(self, name='', ap=None):
        self.name = name
        self.w = {}
        self.r = {}
        self.ap = ap


class Kern:
    def __init__(self, nc, n_dsem_sp=40, n_dsem_pool=40):
        self.nc = nc
        self.E = {'pe': nc.tensor, 'act': nc.scalar, 'dve': nc.vector, 'pool': nc.gpsimd, 'sp': nc.sync}
        self.stack = ExitStack()
        self.sem = {}
        self.cnt = {}
        self.seen = {e: {} for e in self.E}
        for e in self.E:
            self.sem[e] = self.stack.enter_context(nc.semaphore('S_' + e))
            self.cnt[e] = 0
        self.dq = {}
        for q, n in (('sp', n_dsem_sp), ('pool', n_dsem_pool)):
            sems = [self.stack.enter_context(nc.semaphore('D%s%d' % (q, i))) for i in range(n)]
            self.dq[q] = {'sems': sems, 'cnt': [0] * n, 'next': 0}
        self.n_ins = 0
        self.n_wait = 0

    def _semh(self, key):
        if isinstance(key, str):
            return self.sem[key]
        return self.dq[key[0]]['sems'][key[1]]

    def _deps(self, reads, writes, par=False):
        d = {}
        for b in reads:
            for k, v in b.w.items():
                if d.get(k, 0) < v:
                    d[k] = v
        for b in writes:
            for k, v in b.w.items():
                if par and not isinstance(k, str):
                    continue
                if d.get(k, 0) < v:
                    d[k] = v
            for k, v in b.r.items():
                if d.get(k, 0) < v:
                    d[k] = v
        return d

    def _wait(self, e, d):
        eng = self.E[e]
        seen = self.seen[e]
        for k, v in d.items():
            if k == e and not SAME_SYNC[e]:
                continue
            if seen.get(k, 0) >= v:
                continue
            eng.wait_ge(self._semh(k), v)
            seen[k] = v
            self.n_wait += 1

    def _mark(self, tok, reads, writes, par=False):
        k, v = tok
        for b in reads:
            b.r[k] = v
        for b in writes:
            if par:
                b.w = {kk: vv for kk, vv in b.w.items() if not isinstance(kk, str)}
                b.w[k] = v
            else:
                b.w = {k: v}
            b.r = {}

    def op(self, e, fn, reads=(), writes=(), inc=True):
        d = self._deps(reads, writes)
        self._wait(e, d)
        ins = fn()
        self.n_ins += 1
        if inc:
            self.cnt[e] += 1
            ins.then_inc(self.sem[e], 1)
            tok = (e, self.cnt[e])
        else:
            tok = (e, self.cnt[e] + 1)
        self._mark(tok, reads, writes)
        return ins

    def dma(self, q, out, in_, reads=(), writes=(), slow=False, par=True):
        if q == 'sp' and STORES_ON_POOL and type(out.tensor).__name__ == 'DRamTensorHandle' and type(in_.tensor).__name__ != 'DRamTensorHandle':
            q = 'pool'
        d = self._deps(reads, writes, par=par)
        Q = self.dq[q]
        j = Q['next']
        Q['next'] = (j + 1) % len(Q['sems'])
        key = (q, j)
        if Q['cnt'][j] > 0:
            if d.get(key, 0) < Q['cnt'][j]:
                d[key] = Q['cnt'][j]
        self._wait(q, d)
        ins = self.E[q].dma_start(out=out, in_=in_, allow_slow_non_contiguous=True) if slow else self.E[q].dma_start(out=out, in_=in_)
        Q['cnt'][j] += 16
        ins.then_inc(Q['sems'][j], 16)
        self.n_ins += 1
        self._mark((key, Q['cnt'][j]), reads, writes, par=par)
        return ins

    def all_tokens(self):
        d = {}
        for e in self.E:
            if self.cnt[e] > 0:
                d[e] = self.cnt[e]
        for q, Q in self.dq.items():
            for j, c in enumerate(Q['cnt']):
                if c > 0:
                    d[(q, j)] = c
        return d

    def barrier(self, engines=None):
        d = self.all_tokens()
        for e in (engines or self.E):
            dd = {k: v for k, v in d.items() if k != e}
            self._wait(e, dd)

    def mm(self, out_ap, out_buf, pairs, reads):
        n = len(pairs)
        nc = self.nc
        for i, (l, r) in enumerate(pairs):
            self.op('pe', (lambda l=l, r=r, i=i: nc.tensor.matmul(out_ap, lhsT=l, rhs=r, start=(i == 0), stop=(i == n - 1))),
                    reads=reads, writes=[out_buf], inc=(i == n - 1))


def act_recip(K, nc, out_ap, in_ap, reads, out_buf):
    K.op('act', lambda: nc.scalar.activation(out=out_ap, in_=in_ap, func=AF.Ln), reads=reads, writes=[out_buf])
    K.op('act', lambda: nc.scalar.activation(out=out_ap, in_=out_ap, func=AF.Exp, scale=-1.0), reads=[out_buf], writes=[out_buf])


class Phase:
    def __init__(self, K):
        self.K = K
        self.stack = ExitStack()

    def __enter__(self):
        return self

    def __exit__(self, *a):
        self.K.barrier()
        self.stack.close()
        return False

    def sb(self, name, shape, dt):
        self.K.uid = getattr(self.K, 'uid', 0) + 1
        name = "%s_u%d" % (name, self.K.uid)
        t = self.stack.enter_context(self.K.nc.sbuf_tensor(name, list(shape), dt))
        return Buf(name, t)

    def ps(self, name, shape, dt):
        t = self.stack.enter_context(self.K.nc.psum_tensor(name, list(shape), dt))
        return Buf(name, t)


class RR:
    def __init__(self, items):
        self.items = items
        self.i = 0

    def get(self):
        x = self.items[self.i]
        self.i = (self.i + 1) % len(self.items)
        return x

    def peek(self):
        return self.items[self.i]


def cfg_make(S=4096, depth=DEPTH_FULL, debug=False):
    T = CTX + S
    return dict(S=S, T=T, depth=depth, debug=debug, NT=T // 128, NB=T // 256)


def token_groups(cfg, gsz):
    out = [(0, CTX, True)] if gsz >= CTX else [(i, gsz, True) for i in range(0, CTX, gsz)]
    t = CTX
    while t < cfg['T']:
        n = min(gsz, cfg['T'] - t)
        out.append((t, n, False))
        t += n
    return out


def build_program(cfg):
    S, T, depth = cfg['S'], cfg['T'], cfg['depth']
    NT, NB = cfg['NT'], cfg['NB']
    debug = cfg['debug']
    ALPHA = (2 * DEPTH_FULL) ** 0.25
    nc = bass.Bass("TRN2", target_bir_lowering=False)
    K = Kern(nc)
    L = depth

    def dram_in(name, shape, dt=F32):
        return nc.dram_tensor(name, list(shape), dt, kind="ExternalInput").ap()

    skind = "ExternalOutput" if debug else "Internal"

    def dram_scr(name, shape, dt):
        return nc.dram_tensor(name, list(shape), dt, kind=skind).ap()

    xT_in = dram_in("xT_in", [D, T])
    c2T = dram_in("c2T", [D, 2])
    w_ada = dram_in("w_ada", [DEPTH_FULL, D, 6 * D])
    b_ada = dram_in("b_ada", [DEPTH_FULL, 6 * D])
    w_in = dram_in("w_in", [DEPTH_FULL, D, IN_COLS])
    conv_w = dram_in("mlstm_conv_w", [DEPTH_FULL, 5, 512])
    conv_b = dram_in("mlstm_conv_b", [DEPTH_FULL, 512])
    gate_b = dram_in("mlstm_gate_b", [DEPTH_FULL, 16])
    mnorm_g = dram_in("mlstm_norm_g", [DEPTH_FULL, 256])
    dlam = dram_in("diff_lambda", [DEPTH_FULL, 4, 64])
    dnorm_g = dram_in("diff_norm_g", [DEPTH_FULL, 128])
    qn_g = dram_in("gqa_q_norm_g", [DEPTH_FULL, 64])
    kn_g = dram_in("gqa_k_norm_g", [DEPTH_FULL, 64])
    w_out = dram_in("w_out", [DEPTH_FULL, D, D])
    ln1_g = dram_in("ln1_g", [DEPTH_FULL, D])
    ln1_b = dram_in("ln1_b", [DEPTH_FULL, D])
    w_ffn_in = dram_in("w_ffn_in", [DEPTH_FULL, D, 2 * D_FF])
    w_ffn_out = dram_in("w_ffn_out", [DEPTH_FULL, D_FF, D])
    ln2_g = dram_in("ln2_g", [DEPTH_FULL, D])
    ln2_b = dram_in("ln2_b", [DEPTH_FULL, D])
    pp_in = dram_in("pp", [DEPTH_FULL, 128, 64])
    cosT = dram_in("cosT", [128, T])
    sinT = dram_in("sinT", [128, T])
    cmat = dram_in("cmat", [128, 8, 128])
    csel = dram_in("csel", [128, 16])
    outT = nc.dram_tensor("outT", [D, S], F32, kind="ExternalOutput").ap()

    xres = dram_scr("xres", [D, T], F32)
    qkT = dram_scr("qkT", [512, T], F32)
    gT = dram_scr("gT", [16, T], F32)
    mvo = dram_scr("mvt", [T, 256], BF16)
    soT = dram_scr("soT", [256, T], BF16)
    bsc = dram_scr("bsc", [8, T], F32)
    nbrsc = dram_scr("nbrsc", [8 * NT], F32)
    mmask = dram_in("mmask", [128, 8, 512])
    dqkT = dram_scr("dqkT", [1024, T], BF16)
    dvt = dram_scr("dvt", [T, 512], BF16)
    gqkT = dram_scr("gqkT", [384, T], BF16)
    gvt = dram_scr("gvt", [T, 128], BF16)
    yT = dram_scr("yT", [D, T], BF16)
    DB = {n: [Buf('%s_%d' % (n, i)) for i in range(NB)] for n in
          ('xres', 'qkT', 'gT', 'mvo', 'dqkT', 'dvt', 'gqkT', 'gvt', 'yT', 'soT')}
    bsc_buf = Buf('bsc')
    nbrsc_buf = Buf('nbrsc')

    def blks(name, t0, t1):
        t0 = max(t0, 0)
        t1 = min(t1, T)
        return DB[name][t0 // 256:(t1 - 1) // 256 + 1]

    top = ExitStack()

    def psb(name, shape, dt):
        return Buf(name, top.enter_context(nc.sbuf_tensor(name, list(shape), dt)))

    modv = psb("modv", [128, L, 48, 2], F32)
    mod1p = psb("mod1p", [128, L, 48, 2], F32)
    modga = psb("modga", [128, L, 48, 2], F32)
    cm = psb("cm", [128, 8, 128], F32)
    cmb = psb("cmb", [128, 8, 128], BF16)
    cs_ = psb("csel_sb", [128, 16], F32)
    epsc = psb("epsc", [128, 4], F32)
    ppv = psb("ppv", [128, L, 64], F32)
    PS2 = [Buf('psd%d' % i, top.enter_context(nc.psum_tensor('psd%d' % i, [128, 2, 512], F32))) for i in range(2)]
    PS = [Buf('ps%d' % i, PS2[i // 2].ap[:, i % 2, :]) for i in range(4)]
    PS += [Buf('ps%d' % i, top.enter_context(nc.psum_tensor('ps%d' % i, [128, 512], F32))) for i in range(4, 8)]

    ident = cm.ap[:, 0, :]
    identb = cmb.ap[:, 0, :]
    EPS_A = LN_EPS / (ALPHA * ALPHA)

    with Phase(K) as P:
        K.dma('sp', cm.ap[:], cmat, writes=[cm])
        K.op('dve', lambda: nc.vector.memset(epsc.ap[:, 0:1], LN_EPS), writes=[epsc])
        K.op('dve', lambda: nc.vector.memset(epsc.ap[:, 1:2], EPS_A), writes=[epsc])
        K.op('dve', lambda: nc.vector.memset(epsc.ap[:, 2:3], 1.0), writes=[epsc])
        K.dma('sp', cs_.ap[:], csel, writes=[cs_])
        K.op('dve', lambda: nc.vector.tensor_copy(out=cmb.ap[:], in_=cm.ap[:]), reads=[cm], writes=[cmb])
        for l in range(L):
            K.dma('sp', ppv.ap[:, l, :], pp_in[l], writes=[ppv])
        csb = P.sb("c_sb", [128, 8, 2], F32)
        ssb = P.sb("s_sb", [128, 8, 2], F32)
        bada = P.sb("bada", [128, L, 48], F32)
        K.dma('sp', csb.ap[:], c2T.rearrange("(k p) j -> p k j", p=128), writes=[csb], slow=True)
        for l in range(L):
            K.dma('sp', bada.ap[:, l, :], b_ada[l].rearrange("(m p) -> p m", p=128), writes=[bada], slow=True)
        K.op('act', lambda: nc.scalar.activation(out=ssb.ap[:], in_=csb.ap[:], func=AF.Silu), reads=[csb], writes=[ssb])
        wb = [P.sb("wada%d" % i, [128, 8, 1024], F32) for i in range(2)]
        it = 0
        for l in range(L):
            for blk in range(6):
                buf = wb[it % 2]
                it += 1
                for k in range(8):
                    K.dma('sp', buf.ap[:, k, :], w_ada[l, k * 128:(k + 1) * 128, blk * 1024:(blk + 1) * 1024], writes=[buf])
                ps = PS[it % 2]
                for m in range(8):
                    K.mm(ps.ap[:, m * 2:m * 2 + 2], ps,
                         [(buf.ap[:, k, m * 128:(m + 1) * 128], ssb.ap[:, k, :]) for k in range(8)], [buf, ssb])
                K.op('dve', lambda ps=ps, l=l, blk=blk: nc.vector.tensor_tensor(
                    out=modv.ap[:, l, blk * 8:(blk + 1) * 8, :],
                    in0=ps.ap[:, 0:16].rearrange("p (m j) -> p m j", j=2),
                    in1=bada.ap[:, l, blk * 8:(blk + 1) * 8].unsqueeze(2).to_broadcast([128, 8, 2]),
                    op=ALU.add), reads=[ps, bada], writes=[modv])
        K.op('dve', lambda: nc.vector.tensor_scalar(out=mod1p.ap[:], in0=modv.ap[:], scalar1=1.0, scalar2=None, op0=ALU.add),
             reads=[modv], writes=[mod1p])
        K.op('dve', lambda: nc.vector.tensor_scalar(out=modga.ap[:], in0=modv.ap[:], scalar1=1.0 / ALPHA, scalar2=None, op0=ALU.mult),
             reads=[modv], writes=[modga])

    if cfg.get('stop_after') == 'p0':
        dbg = nc.dram_tensor("dbg_mod", [128, L * 96], F32, kind="ExternalOutput").ap()
        K.dma('sp', dbg, modv.ap[:].rearrange("p l m j -> p (l m j)"), reads=[modv])
        K.barrier(engines=['sp'])
        top.close()
        K.stack.close()
        return nc, K

    def modcol(which, l, k, j):
        src = {0: modv, 1: mod1p, 2: modga, 3: modv, 4: mod1p, 5: modga}[which]
        return src, src.ap[:, l, which * 8 + k, j:j + 1]

    for l in range(L):
        xsrc = xT_in if l == 0 else xres
        xsrc_bufs = (lambda t0, t1: []) if l == 0 else (lambda t0, t1: blks('xres', t0, t1))
        with Phase(K) as P:
            w = P.sb("w_in_sb", [128, 8, IN_COLS], BF16)
            wsw = P.sb("w_sw_sb", [128, 8, 1408], BF16)
            for k in range(8):
                K.dma('pool', w.ap[:, k, :], w_in[l, k * 128:(k + 1) * 128, :], writes=[w])
            for (dst0, src0, ncol) in ((0, C_DQ, 1024), (1024, C_GQ, 384)):
                srcv = w.ap[:, :, src0:src0 + ncol].rearrange("p k (g two s) -> p k g two s", two=2, s=16)
                dstv = wsw.ap[:, :, dst0:dst0 + ncol].rearrange("p k (g two s) -> p k g two s", two=2, s=16)
                for k in range(8):
                    K.op('dve', lambda k=k, srcv=srcv, dstv=dstv: nc.vector.tensor_copy(out=dstv[:, k, :, 0, :], in_=srcv[:, k, :, 1, :]),
                         reads=[w], writes=[wsw])
                    K.op('dve', lambda k=k, srcv=srcv, dstv=dstv: nc.vector.tensor_copy(out=dstv[:, k, :, 1, :], in_=srcv[:, k, :, 0, :]),
                         reads=[w], writes=[wsw])
            GS = 512
            xg = [P.sb("xg%d" % i, [128, 8, GS], F32) for i in range(2)]
            xm = [P.sb("xm%d" % i, [128, 8, GS], BF16) for i in range(2)]
            cosb = [P.sb("cos%d" % i, [128, GS], F32) for i in range(2)]
            sinb = [P.sb("sin%d" % i, [128, GS], F32) for i in range(2)]
            st_qk = [P.sb("st_qk%d" % i, [128, 4, GS], F32) for i in range(2)]
            st_g = [P.sb("st_g%d" % i, [16, GS], F32) for i in range(2)]
            st_d = [P.sb("st_d%d" % i, [128, 8, GS], BF16) for i in range(2)]
            st_gq = [P.sb("st_gq%d" % i, [128, 3, GS], BF16) for i in range(2)]
            st_tok = [P.sb("st_tok%d" % i, [128, 896], BF16) for i in range(3)]
            st_so = [P.sb("st_so%d" % i, [128, 2, GS], BF16) for i in range(2)]
            t1b = [P.sb("t1b%d" % i, [128, GS], F32) for i in range(2)]
            t2b = [P.sb("t2b%d" % i, [128, GS], F32) for i in range(2)]
            sqb = [P.sb("sqb%d" % i, [128, GS], BF16) for i in range(2)]
            rb = [P.sb("rb%d" % i, [128, GS], F32) for i in range(2)]
            psr = RR(PS)
            tokr = RR(st_tok)
            tr = RR(list(range(2)))
            for gi, (t0, n, isctx) in enumerate(token_groups(cfg, GS)):
                b = gi % 2
                j = 1 if isctx else 0
                X, XM = xg[b], xm[b]
                for k in range(8):
                    K.dma('sp', X.ap[:, k, :n], xsrc[k * 128:(k + 1) * 128, t0:t0 + n], reads=xsrc_bufs(t0, t0 + n), writes=[X])
                K.dma('sp', cosb[b].ap[:, :n], cosT[:, t0:t0 + n], writes=[cosb[b]])
                K.dma('sp', sinb[b].ap[:, :n], sinT[:, t0:t0 + n], writes=[sinb[b]])
                for k in range(8):
                    sb_, sc_ap = modcol(1, l, k, j)
                    _, sh_ap = modcol(0, l, k, j)
                    K.op('act', lambda k=k, sc_ap=sc_ap, sh_ap=sh_ap, X=X, XM=XM: nc.scalar.activation(
                        out=XM.ap[:, k, :n], in_=X.ap[:, k, :n], func=AF.Identity, scale=sc_ap, bias=sh_ap),
                        reads=[X, mod1p, modv], writes=[XM])

                def fm(col0, m, wt=w):
                    ps = psr.get()
                    K.mm(ps.ap[:m, :n], ps, [(wt.ap[:, k, col0:col0 + m], XM.ap[:, k, :n]) for k in range(8)], [wt, XM])
                    return ps
                parts = cfg.get('parts', 'qk,g,d,gq,tok')
                for mt in range(4 if 'qk' in parts else 0):
                    ps = fm(mt * 128, 128)
                    K.op('act', lambda ps=ps, mt=mt: nc.scalar.copy(out=st_qk[b].ap[:, mt, :n], in_=ps.ap[:, :n]),
                         reads=[ps], writes=[st_qk[b]])
                for mt in range(4 if 'qk' in parts else 0):
                    K.dma('sp', qkT[mt * 128:(mt + 1) * 128, t0:t0 + n], st_qk[b].ap[:, mt, :n], reads=[st_qk[b]], writes=blks('qkT', t0, t0 + n))
                for mt in range(2 if 'qk' in parts else 0):
                    ps = fm(C_MO + mt * 128, 128)
                    K.op('act', lambda ps=ps, mt=mt: nc.scalar.activation(out=st_so[b].ap[:, mt, :n], in_=ps.ap[:, :n], func=AF.Sigmoid),
                         reads=[ps], writes=[st_so[b]])
                for mt in range(2 if 'qk' in parts else 0):
                    K.dma('sp', soT[mt * 128:(mt + 1) * 128, t0:t0 + n], st_so[b].ap[:, mt, :n], reads=[st_so[b]], writes=blks('soT', t0, t0 + n))
                if 'g,' in parts:
                    ps = fm(C_MG, 16)
                    K.op('act', lambda ps=ps: nc.scalar.copy(out=st_g[b].ap[:, :n], in_=ps.ap[:16, :n]), reads=[ps], writes=[st_g[b]])
                    K.dma('sp', gT[:, t0:t0 + n], st_g[b].ap[:, :n], reads=[st_g[b]], writes=blks('gT', t0, t0 + n))
                for mt in range(8 if 'd,' in parts else 0):
                    ps1 = fm(C_DQ + mt * 128, 128)
                    ps2 = fm(mt * 128, 128, wsw)
                    ti = tr.get()
                    K.op('dve', lambda ps1=ps1, ti=ti: nc.vector.tensor_tensor(out=t1b[ti].ap[:, :n], in0=ps1.ap[:, :n], in1=cosb[b].ap[:, :n], op=ALU.mult),
                         reads=[ps1, cosb[b]], writes=[t1b[ti]])
                    K.op('dve', lambda ps2=ps2, ti=ti: nc.vector.tensor_tensor(out=t2b[ti].ap[:, :n], in0=ps2.ap[:, :n], in1=sinb[b].ap[:, :n], op=ALU.mult),
                         reads=[ps2, sinb[b]], writes=[t2b[ti]])
                    K.op('pool', lambda ti=ti, mt=mt: nc.gpsimd.tensor_tensor(out=st_d[b].ap[:, mt, :n], in0=t1b[ti].ap[:, :n], in1=t2b[ti].ap[:, :n], op=ALU.add),
                         reads=[t1b[ti], t2b[ti]], writes=[st_d[b]])
                for mt in range(8 if 'd,' in parts else 0):
                    K.dma('sp', dqkT[mt * 128:(mt + 1) * 128, t0:t0 + n], st_d[b].ap[:, mt, :n], reads=[st_d[b]], writes=blks('dqkT', t0, t0 + n))
                for mt in range(3 if 'gq' in parts else 0):
                    gcol = 0 if mt < 2 else 1
                    ps1 = fm(C_GQ + mt * 128, 128)
                    ps2 = fm(1024 + mt * 128, 128, wsw)
                    ti = tr.get()
                    K.op('act', lambda ps1=ps1, ti=ti: nc.scalar.activation(out=sqb[ti].ap[:, :n], in_=ps1.ap[:, :n], func=AF.Square),
                         reads=[ps1], writes=[sqb[ti]])
                    GQ = cfg.get('gqstep', 9)
                    if GQ < 2:
                        continue
                    ps3 = psr.get()
                    K.mm(ps3.ap[:, :n], ps3, [(cmb.ap[:, 4, :], sqb[ti].ap[:, :n])], [cmb, sqb[ti]])
                    if GQ < 3:
                        continue
                    K.op('act', lambda ps3=ps3, ti=ti: nc.scalar.activation(out=rb[ti].ap[:, :n], in_=ps3.ap[:, :n], func=AF.Ln, bias=epsc.ap[:, 0:1]),
                         reads=[ps3, epsc], writes=[rb[ti]])
                    K.op('act', lambda ti=ti: nc.scalar.activation(out=rb[ti].ap[:, :n], in_=rb[ti].ap[:, :n], func=AF.Exp, scale=-0.5),
                         reads=[rb[ti]], writes=[rb[ti]])
                    if GQ < 4:
                        continue
                    K.op('dve', lambda ps1=ps1, ti=ti, gcol=gcol: nc.vector.scalar_tensor_tensor(
                        out=t1b[ti].ap[:, :n], in0=ps1.ap[:, :n], scalar=ppv.ap[:, l, gcol:gcol + 1], in1=cosb[b].ap[:, :n], op0=ALU.mult, op1=ALU.mult),
                        reads=[ppv, cosb[b]], writes=[t1b[ti], ps1])
                    K.op('dve', lambda ps2=ps2, ti=ti, gcol=gcol: nc.vector.scalar_tensor_tensor(
                        out=t2b[ti].ap[:, :n], in0=ps2.ap[:, :n], scalar=ppv.ap[:, l, 2 + gcol:3 + gcol], in1=sinb[b].ap[:, :n], op0=ALU.mult, op1=ALU.mult),
                        reads=[ps2, ppv, sinb[b]], writes=[t2b[ti]])
                    if GQ < 5:
                        continue
                    K.op('pool', lambda ti=ti: nc.gpsimd.tensor_tensor(out=t1b[ti].ap[:, :n], in0=t1b[ti].ap[:, :n], in1=t2b[ti].ap[:, :n], op=ALU.add),
                         reads=[t1b[ti], t2b[ti]], writes=[t1b[ti]])
                    K.op('dve', lambda ti=ti, mt=mt: nc.vector.tensor_tensor(out=st_gq[b].ap[:, mt, :n], in0=t1b[ti].ap[:, :n], in1=rb[ti].ap[:, :n], op=ALU.mult),
                         reads=[t1b[ti], rb[ti]], writes=[st_gq[b]])
                for mt in range(3 if 'gq' in parts else 0):
                    K.dma('sp', gqkT[mt * 128:(mt + 1) * 128, t0:t0 + n], st_gq[b].ap[:, mt, :n], reads=[st_gq[b]], writes=blks('gqkT', t0, t0 + n))
                for tt in range(n // 128 if 'tok' in parts else 0):
                    ta = t0 + tt * 128
                    stt = tokr.get()
                    for (col0, ncol, dst0) in ((C_MV, 256, 0), (C_DV, 512, 256), (C_GV, 128, 768)):
                        ps = psr.get()
                        K.mm(ps.ap[:, :ncol], ps, [(XM.ap[:, k, tt * 128:(tt + 1) * 128], w.ap[:, k, col0:col0 + ncol]) for k in range(8)], [w, XM])
                        if col0 == C_MV:
                            K.op('act', lambda ps=ps, stt=stt: nc.scalar.copy(out=stt.ap[:, 0:256], in_=ps.ap[:, 0:256]), reads=[ps], writes=[stt])
                        else:
                            K.op('dve', lambda ps=ps, stt=stt, ncol=ncol, dst0=dst0: nc.vector.tensor_copy(out=stt.ap[:, dst0:dst0 + ncol], in_=ps.ap[:, :ncol]), reads=[ps], writes=[stt])
                    K.dma('sp', mvo[ta:ta + 128, :], stt.ap[:, 0:256], reads=[stt], writes=blks('mvo', ta, ta + 128))
                    K.dma('sp', dvt[ta:ta + 128, :], stt.ap[:, 256:768], reads=[stt], writes=blks('dvt', ta, ta + 128))
                    K.dma('sp', gvt[ta:ta + 128, :], stt.ap[:, 768:896], reads=[stt], writes=blks('gvt', ta, ta + 128))
        if cfg.get('stop_after') == 'p1':
            break
        build_attention(K, nc, cfg, l, locals())
        if cfg.get('stop_after') == 'p2':
            break
        build_mlstm(K, nc, cfg, l, locals())
        if cfg.get('stop_after') == 'p3':
            break
        build_ffn(K, nc, cfg, l, locals())

    K.barrier(engines=['sp'])
    top.close()
    K.stack.close()
    return nc, K


def build_attention(K, nc, cfg, l, env):
    T, NT = cfg['T'], cfg['NT']
    PS, cmb, ppv, epsc = env['PS'], env['cmb'], env['ppv'], env['epsc']
    dqkT, dvt, gqkT, gvt, yT, blks, dlam = env['dqkT'], env['dvt'], env['gqkT'], env['gvt'], env['yT'], env['blks'], env['dlam']
    lam_init = 0.8 - 0.6 * math.exp(-0.3 * l)
    if cfg.get('fill_y'):
        for k in range(8):
            K.dma('sp', yT[k * 128:(k + 1) * 128, :], dqkT[k * 128:(k + 1) * 128, :], reads=blks('dqkT', 0, T), writes=blks('yT', 0, T))
        return
    with Phase(K) as P:
        dk = P.sb("dk_sb", [128, 4, T], BF16)
        dv = P.sb("dv_sb", [128, NT, 512], BF16)
        gk = P.sb("gk_sb", [128, T], BF16)
        gv = P.sb("gv_sb", [128, NT, 2, 128], BF16)
        K.op('dve', lambda: nc.vector.memset(gv.ap[:], 1.0), writes=[gv])
        for h in range(4):
            K.dma('sp', dk.ap[:, h, :], dqkT[512 + h * 128:512 + (h + 1) * 128, :], reads=blks('dqkT', 0, T), writes=[dk])
        for kt in range(NT):
            K.dma('sp', dv.ap[:, kt, :], dvt[kt * 128:(kt + 1) * 128, :], reads=blks('dvt', kt * 128, kt * 128 + 128), writes=[dv])
            K.dma('sp', gv.ap[:, kt, 0, 0:64], gvt[kt * 128:(kt + 1) * 128, 0:64], reads=blks('gvt', kt * 128, kt * 128 + 128), writes=[gv])
            K.dma('sp', gv.ap[:, kt, 1, 64:128], gvt[kt * 128:(kt + 1) * 128, 64:128], reads=blks('gvt', kt * 128, kt * 128 + 128), writes=[gv])
        K.dma('sp', gk.ap[:], gqkT[256:384, :], reads=blks('gqkT', 0, T), writes=[gk])
        lv = P.sb("lv", [128, 256], F32)
        lt = P.sb("lt", [128, 256], F32)
        ls = P.sb("ls", [128, 4], F32)
        K.dma('sp', lv.ap[:], dlam[l].rearrange("a d -> (a d)").partition_broadcast(128), writes=[lv])
        K.op('dve', lambda: nc.vector.tensor_tensor(out=lt.ap[:, 0:64], in0=lv.ap[:, 0:64], in1=lv.ap[:, 64:128], op=ALU.mult), reads=[lv], writes=[lt])
        K.op('dve', lambda: nc.vector.tensor_tensor(out=lt.ap[:, 64:128], in0=lv.ap[:, 128:192], in1=lv.ap[:, 192:256], op=ALU.mult), reads=[lv], writes=[lt])
        K.op('dve', lambda: nc.vector.reduce_sum(out=ls.ap[:, 0:1], in_=lt.ap[:, 0:64], axis=AX.X), reads=[lt], writes=[ls])
        K.op('dve', lambda: nc.vector.reduce_sum(out=ls.ap[:, 1:2], in_=lt.ap[:, 64:128], axis=AX.X), reads=[lt], writes=[ls])
        K.op('act', lambda: nc.scalar.activation(out=ls.ap[:, 0:2], in_=ls.ap[:, 0:2], func=AF.Exp), reads=[ls], writes=[ls])
        K.op('dve', lambda: nc.vector.tensor_tensor(out=ls.ap[:, 2:3], in0=ls.ap[:, 1:2], in1=ls.ap[:, 0:1], op=ALU.subtract), reads=[ls], writes=[ls])
        K.op('dve', lambda: nc.vector.tensor_scalar(out=ls.ap[:, 2:3], in0=ls.ap[:, 2:3], scalar1=-lam_init, scalar2=None, op0=ALU.add), reads=[ls], writes=[ls])
        K.op('dve', lambda: nc.vector.tensor_scalar(out=ls.ap[:, 3:4], in0=ppv.ap[:, l, 28:29], scalar1=(1.0 - lam_init), scalar2=None, op0=ALU.mult), reads=[ppv], writes=[ls])
        GS = 512
        dqb = [P.sb("dqb%d" % i, [128, 4, GS], BF16) for i in range(2)]
        gqb = [P.sb("gqb%d" % i, [128, 2, GS], BF16) for i in range(2)]
        NPT = 6
        pt = [P.sb("pt%d" % i, [128, GS], BF16) for i in range(NPT)]
        ptr = RR(pt)
        rsb = P.sb("rsb", [128, GS], F32)
        onb = [P.sb("on%d" % i, [128, GS], F32) for i in range(2)]
        ob = P.sb("ob", [128, GS], F32)
        sqb = P.sb("asq", [128, GS], BF16)
        rrb = P.sb("arr", [128, GS], F32)
        yst = [P.sb("yst%d" % i, [128, GS], BF16) for i in range(2)]
        ystr = RR(yst)
        sdr = RR(env['PS2'])
        pdt = [P.sb("pd%d" % i, [128, 2, GS], BF16) for i in range(3)]
        pdr = RR(pdt)
        accO2 = [PS[4], PS[5]]
        accS2 = [PS[6], PS[7]]
        ssb = PS[6]
        rs2 = P.sb("rs2", [128, GS], F32)
        ones_b = cmb.ap[:, 6, :]
        ones_f = env['cm'].ap[:, 6, :]
        cm_ = env['cm']
        state = {'it': 0}

        def attn_pair(n, kts, streams):
            nk = len(kts)

            def smm(kt):
                SD = sdr.get()
                for a, st in enumerate(streams):
                    K.mm(SD.ap[:, a, :n], SD, [(st['lhs_k'](kt), st['rhs_q'])], st['k_reads'] + st['q_reads'])
                return SD
            cur = smm(kts[0])
            for i, kt in enumerate(kts):
                nxt = smm(kts[i + 1]) if i + 1 < nk else None
                lastk = (i == nk - 1)
                pd = pdr.get()
                K.op('act', lambda cur=cur, pd=pd: nc.scalar.activation(out=pd.ap[:, :, :n], in_=cur.ap[:, :, :n], func=AF.Exp, scale=0.125),
                     reads=[cur], writes=[pd])
                for a, st in enumerate(streams):
                    K.op('pe', lambda kt=kt, pd=pd, a=a, i=i, lastk=lastk, st=st: nc.tensor.matmul(st['accO_ap'], lhsT=st['lhs_v'](kt), rhs=pd.ap[:, a, :n], start=(i == 0), stop=lastk),
                         reads=[pd] + st['v_reads'], writes=[st['accO_buf']], inc=lastk)
                    if st.get('accS_buf') is not None:
                        K.op('pe', lambda pd=pd, a=a, i=i, lastk=lastk, st=st: nc.tensor.matmul(st['accS_buf'].ap[:, :n], lhsT=ones_b, rhs=pd.ap[:, a, :n], start=(i == 0), stop=lastk),
                             reads=[pd, cmb], writes=[st['accS_buf']], inc=lastk)
                cur = nxt

        qblocks = [(0, CTX, True)] + [(t, min(GS, T - t), False) for t in range(CTX, T, GS)]
        for bi, (t0, n, isctx) in enumerate(qblocks):
            b = bi % 2
            kts = list(range(CTX // 128)) if isctx else list(range(NT))
            DQ, GQ = dqb[b], gqb[b]
            for h in range(4):
                K.dma('sp', DQ.ap[:, h, :n], dqkT[h * 128:(h + 1) * 128, t0:t0 + n], reads=blks('dqkT', t0, t0 + n), writes=[DQ])
            for hh in range(4):
                g, jq = hh // 2, hh % 2
                K.dma('sp', GQ.ap[g * 64:(g + 1) * 64, jq, :n], gqkT[hh * 64:(hh + 1) * 64, t0:t0 + n], reads=blks('gqkT', t0, t0 + n), writes=[GQ])
            for h in range(4):
                streams = []
                for m in range(2):
                    base = m * 64
                    streams.append(dict(
                        lhs_k=(lambda kt, h=h, base=base: dk.ap[base:base + 64, h, kt * 128:(kt + 1) * 128]),
                        rhs_q=DQ.ap[base:base + 64, h, :n],
                        lhs_v=(lambda kt, h=h: dv.ap[:, kt, h * 128:(h + 1) * 128]),
                        accO_ap=accO2[m].ap[:, :n], accO_buf=accO2[m],
                        k_reads=[dk], q_reads=[DQ], v_reads=[dv], accS_buf=accS2[m]))
                attn_pair(n, kts, streams)
                for m in range(2):
                    act_recip(K, nc, rsb.ap[:, :n], accS2[m].ap[:, :n], [accS2[m]], rsb)
                    K.op('dve', lambda m=m: nc.vector.tensor_tensor(out=onb[m].ap[:, :n], in0=accO2[m].ap[:, :n], in1=rsb.ap[:, :n], op=ALU.mult),
                         reads=[accO2[m], rsb], writes=[onb[m]])
                K.op('dve', lambda: nc.vector.scalar_tensor_tensor(out=ob.ap[:, :n], in0=onb[1].ap[:, :n], scalar=ls.ap[:, 2:3], in1=onb[0].ap[:, :n], op0=ALU.mult, op1=ALU.add),
                     reads=[onb[0], onb[1], ls], writes=[ob])
                K.op('act', lambda: nc.scalar.activation(out=sqb.ap[:, :n], in_=ob.ap[:, :n], func=AF.Square), reads=[ob], writes=[sqb])
                K.mm(ssb.ap[:, :n], ssb, [(cmb.ap[:, 5, :], sqb.ap[:, :n])], [cmb, sqb])
                K.op('act', lambda: nc.scalar.activation(out=rrb.ap[:, :n], in_=ssb.ap[:, :n], func=AF.Ln, bias=epsc.ap[:, 0:1]), reads=[ssb, epsc], writes=[rrb])
                K.op('act', lambda: nc.scalar.activation(out=rrb.ap[:, :n], in_=rrb.ap[:, :n], func=AF.Exp, scale=-0.5), reads=[rrb], writes=[rrb])
                ys = ystr.get()
                K.op('dve', lambda ys=ys: nc.vector.scalar_tensor_tensor(out=ys.ap[:, :n], in0=ob.ap[:, :n], scalar=ls.ap[:, 3:4], in1=rrb.ap[:, :n], op0=ALU.mult, op1=ALU.mult),
                     reads=[ob, ls, rrb], writes=[ys])
                K.dma('sp', yT[256 + h * 128:256 + (h + 1) * 128, t0:t0 + n], ys.ap[:, :n], reads=[ys], writes=blks('yT', t0, t0 + n))
            for jq in range(2):
                streams = []
                for g in range(2):
                    base = g * 64
                    streams.append(dict(
                        lhs_k=(lambda kt, base=base: gk.ap[base:base + 64, kt * 128:(kt + 1) * 128]),
                        rhs_q=GQ.ap[base:base + 64, jq, :n],
                        lhs_v=(lambda kt, g=g: gv.ap[:, kt, g, :]),
                        accO_ap=accO2[g].ap[:, :n], accO_buf=accO2[g],
                        k_reads=[gk], q_reads=[GQ], v_reads=[gv]))
                attn_pair(n, kts, streams)
                K.op('dve', lambda: nc.vector.reciprocal(out=rsb.ap[64:128, :n], in_=accO2[0].ap[64:128, :n]), reads=[accO2[0]], writes=[rsb])
                K.op('dve', lambda: nc.vector.reciprocal(out=rsb.ap[0:64, :n], in_=accO2[1].ap[0:64, :n]), reads=[accO2[1]], writes=[rsb])
                K.dma('sp', rs2.ap[0:64, :n], rsb.ap[64:128, :n], reads=[rsb], writes=[rs2])
                K.dma('sp', rs2.ap[64:128, :n], rsb.ap[0:64, :n], reads=[rsb], writes=[rs2])
                ys = ystr.get()
                K.op('dve', lambda ys=ys: nc.vector.tensor_tensor(out=ys.ap[0:64, :n], in0=accO2[0].ap[0:64, :n], in1=rs2.ap[0:64, :n], op=ALU.mult),
                     reads=[accO2[0], rs2], writes=[ys])
                K.op('dve', lambda ys=ys: nc.vector.tensor_tensor(out=ys.ap[64:128, :n], in0=accO2[1].ap[64:128, :n], in1=rs2.ap[64:128, :n], op=ALU.mult),
                     reads=[accO2[1], rs2], writes=[ys])
                for g in range(2):
                    hh = g * 2 + jq
                    K.dma('sp', yT[768 + hh * 64:768 + (hh + 1) * 64, t0:t0 + n], ys.ap[g * 64:(g + 1) * 64, :n], reads=[ys], writes=blks('yT', t0, t0 + n))


def build_mlstm(K, nc, cfg, l, env):
    T, NT = cfg['T'], cfg['NT']
    PS, cm, cmb, ppv, epsc, cs_ = env['PS'], env['cm'], env['cmb'], env['ppv'], env['epsc'], env['cs_']
    qkT, gT, mvo, soT, yT, blks = env['qkT'], env['gT'], env['mvo'], env['soT'], env['yT'], env['blks']
    bsc, nbrsc, bsc_buf, nbrsc_buf, mmask, gate_b = env['bsc'], env['nbrsc'], env['bsc_buf'], env['nbrsc_buf'], env['mmask'], env['gate_b']
    with Phase(K) as P:
        wkt = P.sb("wkt", [128, NT, 8], F32)
        nbr = P.sb("nbr", [128, 8 * NT], F32)
        with Phase(K) as G:
            GI = G.sb("GI", [8, T], F32)
            GF = G.sb("GF", [8, T], F32)
            CF = G.sb("CF", [8, T], F32)
            BB = G.sb("BB", [8, T], F32)
            ON = G.sb("ON", [8, T], F32)
            gb = G.sb("gb", [8, 4], F32)
            tot = G.sb("tot", [8, 2], F32)
            BR = G.sb("BR", [8, NT], F32)
            K.dma('sp', GI.ap[:], gT[0:8, :], reads=blks('gT', 0, T), writes=[GI])
            K.dma('sp', GF.ap[:], gT[8:16, :], reads=blks('gT', 0, T), writes=[GF])
            K.dma('sp', gb.ap[:, 0:1], gate_b[l, 0:8].rearrange("(p o) -> p o", o=1), writes=[gb], slow=True)
            K.dma('sp', gb.ap[:, 1:2], gate_b[l, 8:16].rearrange("(p o) -> p o", o=1), writes=[gb], slow=True)
            K.op('dve', lambda: nc.vector.tensor_scalar(out=gb.ap[:, 2:3], in0=gb.ap[:, 1:2], scalar1=-1.0, scalar2=None, op0=ALU.mult), reads=[gb], writes=[gb])
            K.op('dve', lambda: nc.vector.memset(ON.ap[:], 1.0), writes=[ON])
            K.op('act', lambda: nc.scalar.activation(out=GF.ap[:], in_=GF.ap[:], func=AF.Exp, scale=-1.0, bias=gb.ap[:, 2:3]), reads=[GF, gb], writes=[GF])
            K.op('act', lambda: nc.scalar.activation(out=GF.ap[:], in_=GF.ap[:], func=AF.Ln, bias=epsc.ap[:8, 2:3]), reads=[GF, epsc], writes=[GF])
            K.op('dve', lambda: nc.vector.tensor_scalar(out=GF.ap[:], in0=GF.ap[:], scalar1=-1.0, scalar2=None, op0=ALU.mult), reads=[GF], writes=[GF])
            K.op('dve', lambda: nc.vector.tensor_scalar(out=GI.ap[:], in0=GI.ap[:], scalar1=gb.ap[:, 0:1], scalar2=None, op0=ALU.add), reads=[GI, gb], writes=[GI])
            K.op('dve', lambda: nc.vector.tensor_tensor_scan(out=CF.ap[:], data0=ON.ap[:], data1=GF.ap[:], initial=0.0, op0=ALU.mult, op1=ALU.add),
                 reads=[ON, GF], writes=[CF])
            K.op('dve', lambda: nc.vector.tensor_copy(out=tot.ap[:, 0:1], in_=CF.ap[:, CTX - 1:CTX]), reads=[CF], writes=[tot])
            K.op('dve', lambda: nc.vector.tensor_tensor(out=tot.ap[:, 1:2], in0=CF.ap[:, CTX - 1:CTX], in1=CF.ap[:, T - 1:T], op=ALU.add), reads=[CF], writes=[tot])
            K.op('dve', lambda: nc.vector.tensor_tensor(out=BB.ap[:], in0=GF.ap[:], in1=CF.ap[:], op=ALU.subtract), reads=[GF, CF], writes=[BB])
            K.op('dve', lambda: nc.vector.tensor_scalar(out=BB.ap[:, :CTX], in0=BB.ap[:, :CTX], scalar1=tot.ap[:, 0:1], scalar2=None, op0=ALU.add), reads=[BB, tot], writes=[BB])
            K.op('dve', lambda: nc.vector.tensor_scalar(out=BB.ap[:, CTX:], in0=BB.ap[:, CTX:], scalar1=tot.ap[:, 1:2], scalar2=None, op0=ALU.add), reads=[BB, tot], writes=[BB])
            K.op('dve', lambda: nc.vector.tensor_scalar(out=BB.ap[:], in0=BB.ap[:], scalar1=cs_.ap[:8, 1:2], scalar2=None, op0=ALU.mult), reads=[BB, cs_], writes=[BB])
            K.op('dve', lambda: nc.vector.scalar_tensor_tensor(out=BB.ap[:], in0=CF.ap[:], scalar=cs_.ap[:8, 0:1], in1=BB.ap[:], op0=ALU.mult, op1=ALU.add),
                 reads=[CF, cs_, BB], writes=[BB])
            Bv = BB.ap[:].rearrange("p (kt s) -> p kt s", s=128)
            K.op('dve', lambda: nc.vector.tensor_scalar(out=BR.ap[:], in0=Bv[:, :, 0], scalar1=cs_.ap[:8, 1:2], scalar2=None, op0=ALU.mult), reads=[BB, cs_], writes=[BR])
            K.op('dve', lambda: nc.vector.scalar_tensor_tensor(out=BR.ap[:], in0=Bv[:, :, 127], scalar=cs_.ap[:8, 0:1], in1=BR.ap[:], op0=ALU.mult, op1=ALU.add),
                 reads=[BB, cs_, BR], writes=[BR])
            K.op('dve', lambda: nc.vector.tensor_tensor(out=GI.ap[:], in0=GI.ap[:], in1=BB.ap[:], op=ALU.subtract), reads=[GI, BB], writes=[GI])
            GIv = GI.ap[:].rearrange("p (kt s) -> p kt s", s=128)
            K.op('dve', lambda: nc.vector.tensor_tensor(out=GIv, in0=GIv, in1=BR.ap[:].unsqueeze(2).to_broadcast([8, NT, 128]), op=ALU.add), reads=[GI, BR], writes=[GI])
            K.op('act', lambda: nc.scalar.activation(out=GI.ap[:], in_=GI.ap[:], func=AF.Exp), reads=[GI], writes=[GI])
            K.op('dve', lambda: nc.vector.tensor_scalar(out=BR.ap[:], in0=BR.ap[:], scalar1=-1.0, scalar2=None, op0=ALU.mult), reads=[BR], writes=[BR])
            K.dma('sp', bsc, BB.ap[:], reads=[BB], writes=[bsc_buf])
            K.dma('sp', nbrsc.rearrange("(h k) -> h k", k=NT), BR.ap[:], reads=[BR], writes=[nbrsc_buf])
            pst = PS[7]
            for kt in range(NT):
                K.op('pe', lambda kt=kt: nc.tensor.transpose(out=pst.ap[:, kt * 8:(kt + 1) * 8], in_=GI.ap[:, kt * 128:(kt + 1) * 128], identity=cm.ap[:8, 0, :8]),
                     reads=[GI, cm], writes=[pst], inc=(kt == NT - 1))
            K.op('dve', lambda: nc.vector.tensor_copy(out=wkt.ap[:].rearrange("p k h -> p (k h)"), in_=pst.ap[:, :NT * 8]), reads=[pst], writes=[wkt])
            K.dma('sp', nbr.ap[:], nbrsc.partition_broadcast(128), reads=[nbrsc_buf], writes=[nbr])
        qc = P.sb("qc", [128, 2, T], BF16)
        kc = P.sb("kc", [128, 2, T], BF16)
        ub = [P.sb("ub%d" % i, [128, 4, 516], F32) for i in range(2)]
        accb = [P.sb("cacc%d" % i, [128, 512], F32) for i in range(2)]
        pieces = [(0, CTX, 0, CTX)] + [(t, min(512, T - t), CTX, T) for t in range(CTX, T, 512)]
        for pi, (t0, n, s0, s1) in enumerate(pieces):
            U = ub[pi % 2]
            lo, hi = max(s0, t0 - 2), min(s1, t0 + n + 2)
            K.op('pool', lambda U=U: nc.gpsimd.memset(U.ap[:], 0.0), writes=[U])
            for mt in range(4):
                K.dma('sp', U.ap[:, mt, lo - (t0 - 2):hi - (t0 - 2)], qkT[mt * 128:(mt + 1) * 128, lo:hi], reads=blks('qkT', lo, hi), writes=[U])
            for mt in range(4):
                A = accb[mt % 2]
                wcol = 8 + mt * 5
                K.op('act', lambda U=U, A=A, mt=mt, wcol=wcol: nc.scalar.activation(out=A.ap[:, :n], in_=U.ap[:, mt, 0:n], func=AF.Identity, scale=ppv.ap[:, l, wcol:wcol + 1]),
                     reads=[U, ppv], writes=[A])
                for jj in range(1, 5):
                    K.op('dve', lambda U=U, A=A, mt=mt, jj=jj, wcol=wcol: nc.vector.scalar_tensor_tensor(
                        out=A.ap[:, :n], in0=U.ap[:, mt, jj:jj + n], scalar=ppv.ap[:, l, wcol + jj:wcol + jj + 1], in1=A.ap[:, :n], op0=ALU.mult, op1=ALU.add),
                        reads=[U, ppv, A], writes=[A])
                if mt < 2:
                    K.op('act', lambda A=A, mt=mt: nc.scalar.activation(out=A.ap[:, :n], in_=A.ap[:, :n], func=AF.Silu, bias=ppv.ap[:, l, 4 + mt:5 + mt]), reads=[A, ppv], writes=[A])
                    K.op('dve', lambda A=A, mt=mt: nc.vector.tensor_scalar(out=qc.ap[:, mt, t0:t0 + n], in0=A.ap[:, :n], scalar1=0.125, scalar2=None, op0=ALU.mult), reads=[A], writes=[qc])
                else:
                    K.op('act', lambda A=A, mt=mt: nc.scalar.activation(out=kc.ap[:, mt - 2, t0:t0 + n], in_=A.ap[:, :n], func=AF.Silu, bias=ppv.ap[:, l, 4 + mt:5 + mt]),
                         reads=[A, ppv], writes=[kc])
        mv = P.sb("mv_sb", [128, NT, 4, 128], BF16)
        K.op('dve', lambda: nc.vector.memset(mv.ap[:], 1.0), writes=[mv])
        for kt in range(NT):
            for h in range(4):
                c0 = (h % 2) * 64
                K.dma('sp', mv.ap[:, kt, h, c0:c0 + 64], mvo[kt * 128:(kt + 1) * 128, h * 64:(h + 1) * 64], reads=blks('mvo', kt * 128, kt * 128 + 128), writes=[mv])
        mk = P.sb("mmask_sb", [128, 8, 512], BF16)
        K.dma('pool', mk.ap[:], mmask, writes=[mk])
        GS = 512
        NPT = 6
        pt = [P.sb("mpt%d" % i, [128, GS], BF16) for i in range(NPT)]
        ptr = RR(pt)
        pmb = [P.sb("mpm%d" % i, [128, GS], BF16) for i in range(2)]
        pmr = RR(pmb)
        etb = [P.sb("met%d" % i, [128, GS], F32) for i in range(4)]
        etr = RR(etb)
        bqb = [P.sb("mbq%d" % i, [128, GS], F32) for i in range(3)]
        bqr = RR(bqb)
        sob = [P.sb("mso%d" % i, [128, GS], BF16) for i in range(2)]
        hT = [P.sb("mhT%d" % i, [128, GS], F32) for i in range(2)]
        ddb = P.sb("mdd", [128, GS], F32)
        dd2 = P.sb("mdd2", [128, GS], F32)
        hs = P.sb("mhs", [128, GS], F32)
        hbb = P.sb("mhb", [128, GS], BF16)
        hsq = P.sb("mhsq", [128, GS], BF16)
        mean_sb = P.sb("mmean", [128, GS], F32)
        m2 = P.sb("mm2", [128, GS], F32)
        rstd = P.sb("mrstd", [128, GS], F32)
        yst = [P.sb("myst%d" % i, [128, GS], BF16) for i in range(2)]
        sbank = RR(PS[0:3])
        accr = RR([(PS[3], PS[4]), (PS[5], PS[6])])
        ones = cmb.ap[:, 6, :]
        qblocks = [(0, CTX, True)] + [(t, min(GS, T - t), False) for t in range(CTX, T, GS)]

        def key_tiles(d, t0, n, isctx):
            out = []
            if isctx:
                return [(kt, kt * 128) for kt in range(CTX // 128)]
            if d == 0:
                for kt in range(NT):
                    if (kt + 1) * 128 <= t0:
                        out.append((kt, None))
                    elif kt * 128 < t0 + n:
                        out.append((kt, kt * 128 - t0))
            else:
                for kt in range(NT):
                    if kt < CTX // 128:
                        out.append((kt, None))
                    elif kt * 128 >= t0 + n:
                        out.append((kt, None))
                    elif (kt + 1) * 128 > t0:
                        out.append((kt, kt * 128 - t0))
            return out

        msb = [RR([PS[0], PS[1]]), RR([PS[2], PS[3]])]
        maccr = RR([(PS[4], PS[5]), (PS[6], PS[7])])
        ones_f = cm.ap[:, 6, :]
        bq_sets = [[P.sb("mbqs%d_%d" % (a, c), [128, GS], F32) for c in range(4)] for a in range(2)]
        iters = [(j, t0, n, isctx) for j in range(2) for (t0, n, isctx) in qblocks]

        def load_bq(ii):
            j_, t0_, n_, _ = iters[ii]
            for d_ in range(2):
                for hp_ in range(2):
                    hd_ = d_ * 4 + 2 * j_ + hp_
                    BQ_ = bq_sets[ii % 2][d_ * 2 + hp_]
                    K.dma('sp', BQ_.ap[:, :n_], bsc[hd_, t0_:t0_ + n_].partition_broadcast(128), reads=[bsc_buf], writes=[BQ_])
        deferred = []

        def run_deferred(i, force):
            for item in list(deferred):
                if force or i >= item[0]:
                    deferred.remove(item)
                    item[1]()
        load_bq(0)
        for ii, (j, t0, n, isctx) in enumerate(iters):
            bi = ii + 1
            if True:
                if ii + 1 < len(iters):
                    load_bq(ii + 1)
                SO = sob[bi % 2]
                K.dma('sp', SO.ap[:, :n], soT[j * 128:(j + 1) * 128, t0:t0 + n], reads=blks('soT', t0, t0 + n), writes=[SO])
                for d in range(2):
                    accN, accD = maccr.get()
                    kts = key_tiles(d, t0, n, isctx)
                    nk = len(kts)
                    BQs = [bq_sets[ii % 2][d * 2 + hp] for hp in range(2)]

                    def smm(kt):
                        out = []
                        for hp in range(2):
                            base = hp * 64
                            sb_ = msb[hp].get()
                            K.mm(sb_.ap[:, :n], sb_, [(kc.ap[base:base + 64, j, kt * 128:(kt + 1) * 128], qc.ap[base:base + 64, j, t0:t0 + n])], [kc, qc])
                            out.append(sb_)
                        return out
                    cur = smm(kts[0][0])
                    for i, (kt, off) in enumerate(kts):
                        lastk = (i == nk - 1)
                        run_deferred(i, lastk)
                        nxt = smm(kts[i + 1][0]) if i + 1 < nk else None
                        for hp in range(2):
                            h = 2 * j + hp
                            hd = d * 4 + h
                            BQ = BQs[hp]
                            sb_cur = cur[hp]
                            et = etr.get()
                            if off is None:
                                K.op('act', lambda et=et, BQ=BQ, kt=kt, hd=hd: nc.scalar.activation(out=et.ap[:, :n], in_=BQ.ap[:, :n], func=AF.Exp, bias=nbr.ap[:, hd * NT + kt:hd * NT + kt + 1]),
                                     reads=[BQ, nbr], writes=[et])
                            else:
                                K.op('dve', lambda et=et, BQ=BQ, kt=kt, hd=hd: nc.vector.tensor_scalar(out=et.ap[:, :n], in0=BQ.ap[:, :n], scalar1=nbr.ap[:, hd * NT + kt:hd * NT + kt + 1],
                                                                                                    scalar2=60.0, op0=ALU.add, op1=ALU.min), reads=[BQ, nbr], writes=[et])
                                K.op('act', lambda et=et: nc.scalar.activation(out=et.ap[:, :n], in_=et.ap[:, :n], func=AF.Exp), reads=[et], writes=[et])
                            p = ptr.get()
                            if off is None:
                                K.op('dve', lambda sb_cur=sb_cur, et=et, p=p, kt=kt, hd=hd: nc.vector.scalar_tensor_tensor(
                                    out=p.ap[:, :n], in0=sb_cur.ap[:, :n], scalar=wkt.ap[:, kt, hd:hd + 1], in1=et.ap[:, :n], op0=ALU.mult, op1=ALU.mult),
                                    reads=[sb_cur, wkt, et], writes=[p])
                            else:
                                pm_ = pmr.get()
                                K.op('dve', lambda sb_cur=sb_cur, et=et, pm_=pm_, kt=kt, hd=hd: nc.vector.scalar_tensor_tensor(
                                    out=pm_.ap[:, :n], in0=sb_cur.ap[:, :n], scalar=wkt.ap[:, kt, hd:hd + 1], in1=et.ap[:, :n], op0=ALU.mult, op1=ALU.mult),
                                    reads=[sb_cur, wkt, et], writes=[pm_])
                                mi = d * 4 + off // 128
                                K.op('dve', lambda pm_=pm_, p=p, mi=mi: nc.vector.tensor_tensor(out=p.ap[:, :n], in0=pm_.ap[:, :n], in1=mk.ap[:, mi, :n], op=ALU.mult),
                                     reads=[pm_, mk], writes=[p])
                            acc_hp = accN if hp == 0 else accD
                            K.op('pe', lambda kt=kt, p=p, i=i, lastk=lastk, h=h, acc_hp=acc_hp: nc.tensor.matmul(acc_hp.ap[:, :n], lhsT=mv.ap[:, kt, h, :], rhs=p.ap[:, :n], start=(i == 0), stop=lastk),
                                 reads=[p, mv], writes=[acc_hp], inc=lastk)
                        cur = nxt

                    def tail(accN=accN, accD=accD, d=d, n=n):
                        for (bk, r0) in ((accN, 64), (accD, 0)):
                            K.op('dve', lambda bk=bk, r0=r0: nc.vector.tensor_scalar(out=ddb.ap[r0:r0 + 64, :n], in0=bk.ap[r0:r0 + 64, :n], scalar1=-1.0, scalar2=1.0, op0=ALU.mult, op1=ALU.max),
                                 reads=[bk], writes=[ddb])
                            K.op('dve', lambda bk=bk, r0=r0: nc.vector.tensor_tensor(out=ddb.ap[r0:r0 + 64, :n], in0=bk.ap[r0:r0 + 64, :n], in1=ddb.ap[r0:r0 + 64, :n], op=ALU.max),
                                 reads=[bk, ddb], writes=[ddb])
                        act_recip(K, nc, ddb.ap[:, :n], ddb.ap[:, :n], [ddb], ddb)
                        K.dma('sp', dd2.ap[0:64, :n], ddb.ap[64:128, :n], reads=[ddb], writes=[dd2])
                        K.dma('sp', dd2.ap[64:128, :n], ddb.ap[0:64, :n], reads=[ddb], writes=[dd2])

                    def tail2(accN=accN, accD=accD, d=d, n=n):
                        K.op('dve', lambda: nc.vector.tensor_tensor(out=hT[d].ap[0:64, :n], in0=accN.ap[0:64, :n], in1=dd2.ap[0:64, :n], op=ALU.mult),
                             reads=[accN, dd2], writes=[hT[d]])
                        K.op('dve', lambda: nc.vector.tensor_tensor(out=hT[d].ap[64:128, :n], in0=accD.ap[64:128, :n], in1=dd2.ap[64:128, :n], op=ALU.mult),
                             reads=[accD, dd2], writes=[hT[d]])
                    deferred.append([1, tail])
                    deferred.append([4, tail2])

                def ln_block(j=j, t0=t0, n=n, SO=SO, bi=bi):
                    K.op('dve', lambda: nc.vector.tensor_tensor(out=hs.ap[:, :n], in0=hT[0].ap[:, :n], in1=hT[1].ap[:, :n], op=ALU.add), reads=[hT[0], hT[1]], writes=[hs])
                    K.op('act', lambda: nc.scalar.copy(out=hbb.ap[:, :n], in_=hs.ap[:, :n]), reads=[hs], writes=[hbb])
                    K.op('act', lambda: nc.scalar.activation(out=hsq.ap[:, :n], in_=hs.ap[:, :n], func=AF.Square), reads=[hs], writes=[hsq])
                    pm = msb[0].peek()
                    K.mm(pm.ap[:, :n], pm, [(cmb.ap[:, 4, :], hbb.ap[:, :n])], [cmb, hbb])
                    pq = msb[1].peek()
                    K.mm(pq.ap[:, :n], pq, [(cmb.ap[:, 4, :], hsq.ap[:, :n])], [cmb, hsq])
                    K.op('act', lambda: nc.scalar.copy(out=mean_sb.ap[:, :n], in_=pm.ap[:, :n]), reads=[pm], writes=[mean_sb])
                    K.op('dve', lambda: nc.vector.tensor_tensor(out=m2.ap[:, :n], in0=mean_sb.ap[:, :n], in1=mean_sb.ap[:, :n], op=ALU.mult), reads=[mean_sb], writes=[m2])
                    K.op('dve', lambda: nc.vector.tensor_tensor(out=m2.ap[:, :n], in0=pq.ap[:, :n], in1=m2.ap[:, :n], op=ALU.subtract), reads=[pq, m2], writes=[m2])
                    K.op('act', lambda: nc.scalar.activation(out=rstd.ap[:, :n], in_=m2.ap[:, :n], func=AF.Ln, bias=epsc.ap[:, 0:1]), reads=[m2, epsc], writes=[rstd])
                    K.op('act', lambda: nc.scalar.activation(out=rstd.ap[:, :n], in_=rstd.ap[:, :n], func=AF.Exp, scale=-0.5), reads=[rstd], writes=[rstd])
                    K.op('dve', lambda: nc.vector.tensor_tensor(out=hs.ap[:, :n], in0=hs.ap[:, :n], in1=mean_sb.ap[:, :n], op=ALU.subtract), reads=[hs, mean_sb], writes=[hs])
                    K.op('dve', lambda: nc.vector.tensor_tensor(out=hs.ap[:, :n], in0=hs.ap[:, :n], in1=rstd.ap[:, :n], op=ALU.mult), reads=[hs, rstd], writes=[hs])
                    ys = yst[bi % 2]
                    K.op('dve', lambda: nc.vector.scalar_tensor_tensor(out=ys.ap[:, :n], in0=hs.ap[:, :n], scalar=ppv.ap[:, l, 29 + j:30 + j], in1=SO.ap[:, :n], op0=ALU.mult, op1=ALU.mult),
                         reads=[hs, ppv, SO], writes=[ys])
                    K.dma('sp', yT[j * 128:(j + 1) * 128, t0:t0 + n], ys.ap[:, :n], reads=[ys], writes=blks('yT', t0, t0 + n))
                deferred.append([7, ln_block])
        run_deferred(0, True)


def build_ffn(K, nc, cfg, l, env):
    PS, cmb, ppv, epsc, modcol = env['PS'], env['cmb'], env['ppv'], env['epsc'], env['modcol']
    modv, mod1p, modga = env['modv'], env['mod1p'], env['modga']
    xres, yT, outT, blks = env['xres'], env['yT'], env['outT'], env['blks']
    w_out, w_ffn_in, w_ffn_out, xT_in = env['w_out'], env['w_ffn_in'], env['w_ffn_out'], env['xT_in']
    L = cfg['depth']
    last = (l == L - 1)
    xsrc = xT_in if l == 0 else xres
    xsrc_bufs = (lambda t0, t1: []) if l == 0 else (lambda t0, t1: blks('xres', t0, t1))
    psr = RR(PS)

    def layer_norm(P, R, n, gcol0, bcol0, OUT, tmpA, tmpB, Rb, SQ, mean_sb, m2, rstd, post=None):
        K.op('act', lambda: nc.scalar.copy(out=Rb.ap[:, :, :n], in_=R.ap[:, :, :n]), reads=[R], writes=[Rb])
        K.op('act', lambda: nc.scalar.activation(out=SQ.ap[:, :, :n], in_=R.ap[:, :, :n], func=AF.Square), reads=[R], writes=[SQ])
        pm = psr.get()
        K.mm(pm.ap[:, :n], pm, [(cmb.ap[:, 3, :], Rb.ap[:, m, :n]) for m in range(8)], [cmb, Rb])
        pq = psr.get()
        K.mm(pq.ap[:, :n], pq, [(cmb.ap[:, 3, :], SQ.ap[:, m, :n]) for m in range(8)], [cmb, SQ])
        K.op('act', lambda: nc.scalar.copy(out=mean_sb.ap[:, :n], in_=pm.ap[:, :n]), reads=[pm], writes=[mean_sb])
        K.op('dve', lambda: nc.vector.tensor_tensor(out=m2.ap[:, :n], in0=mean_sb.ap[:, :n], in1=mean_sb.ap[:, :n], op=ALU.mult), reads=[mean_sb], writes=[m2])
        K.op('dve', lambda: nc.vector.tensor_tensor(out=m2.ap[:, :n], in0=pq.ap[:, :n], in1=m2.ap[:, :n], op=ALU.subtract), reads=[pq, m2], writes=[m2])
        K.op('act', lambda: nc.scalar.activation(out=rstd.ap[:, :n], in_=m2.ap[:, :n], func=AF.Ln, bias=epsc.ap[:, 1:2]), reads=[m2, epsc], writes=[rstd])
        K.op('act', lambda: nc.scalar.activation(out=rstd.ap[:, :n], in_=rstd.ap[:, :n], func=AF.Exp, scale=-0.5), reads=[rstd], writes=[rstd])
        K.op('dve', lambda: nc.vector.tensor_tensor(out=R.ap[:, :, :n], in0=R.ap[:, :, :n], in1=mean_sb.ap[:, :n].unsqueeze(1).to_broadcast([128, 8, n]), op=ALU.subtract),
             reads=[R, mean_sb], writes=[R])
        K.op('dve', lambda: nc.vector.tensor_tensor(out=R.ap[:, :, :n], in0=R.ap[:, :, :n], in1=rstd.ap[:, :n].unsqueeze(1).to_broadcast([128, 8, n]), op=ALU.mult),
             reads=[R, rstd], writes=[R])
        for m in range(8):
            K.op('act', lambda m=m: nc.scalar.activation(out=OUT.ap[:, m, :n], in_=R.ap[:, m, :n], func=AF.Identity,
                                                        scale=ppv.ap[:, l, gcol0 + m:gcol0 + m + 1], bias=ppv.ap[:, l, bcol0 + m:bcol0 + m + 1]),
                 reads=[R, ppv], writes=[OUT])
            if post is not None:
                post(m)

    with Phase(K) as P:
        wo = P.sb("wo_sb", [128, 8, 1024], BF16)
        for k in range(8):
            K.dma('pool', wo.ap[:, k, :], w_out[l, k * 128:(k + 1) * 128, :], writes=[wo])
        GS = 512
        Xb = [P.sb("fx%d" % i, [128, 8, GS], F32) for i in range(2)]
        Yb = [P.sb("fy%d" % i, [128, 8, GS], BF16) for i in range(2)]
        Rr = [P.sb("fr%d" % i, [128, 8, GS], F32) for i in range(2)]
        Rb = P.sb("frb", [128, 8, GS], BF16)
        SQ = P.sb("fsq", [128, 8, GS], BF16)
        tmpA = P.sb("ftA", [128, GS], F32)
        tmpB = P.sb("ftB", [128, GS], F32)
        mean_sb = P.sb("fmean", [128, GS], F32)
        m2 = P.sb("fm2", [128, GS], F32)
        rstd = P.sb("frstd", [128, GS], F32)
        for gi, (t0, n, isctx) in enumerate(token_groups(cfg, GS)):
            b = gi % 2
            j = 1 if isctx else 0
            X, Y = Xb[b], Yb[b]
            R = Rr[b]
            for k in range(8):
                K.dma('sp', X.ap[:, k, :n], xsrc[k * 128:(k + 1) * 128, t0:t0 + n], reads=xsrc_bufs(t0, t0 + n), writes=[X])
                K.dma('sp', Y.ap[:, k, :n], yT[k * 128:(k + 1) * 128, t0:t0 + n], reads=blks('yT', t0, t0 + n), writes=[Y])
            for m in range(8):
                ps = psr.get()
                K.mm(ps.ap[:, :n], ps, [(wo.ap[:, k, m * 128:(m + 1) * 128], Y.ap[:, k, :n]) for k in range(8)], [wo, Y])
                gsrc, gap = modcol(2, l, m, j)
                K.op('dve', lambda ps=ps, m=m, gap=gap, X=X: nc.vector.scalar_tensor_tensor(
                    out=R.ap[:, m, :n], in0=ps.ap[:, :n], scalar=gap, in1=X.ap[:, m, :n], op0=ALU.mult, op1=ALU.add),
                    reads=[ps, gsrc, X], writes=[R])
            layer_norm(P, R, n, 32, 40, X, tmpA, tmpB, Rb, SQ, mean_sb, m2, rstd)
            for k in range(8):
                K.dma('sp', xres[k * 128:(k + 1) * 128, t0:t0 + n], X.ap[:, k, :n], reads=[X], writes=blks('xres', t0, t0 + n))

    with Phase(K) as P:
        wfi = P.sb("wfi_sb", [128, 8, 2 * D_FF], BF16)
        wfo = P.sb("wfo_sb", [128, 22, 1024], BF16)
        for k in range(8):
            K.dma('pool', wfi.ap[:, k, :], w_ffn_in[l, k * 128:(k + 1) * 128, :], writes=[wfi])
        for k in range(22):
            K.dma('pool', wfo.ap[:, k, :], w_ffn_out[l, k * 128:(k + 1) * 128, :], writes=[wfo])
        GS = 256
        Xb = [P.sb("gx%d" % i, [128, 8, GS], F32) for i in range(2)]
        XM = P.sb("gxm", [128, 8, GS], BF16)
        AC = P.sb("gac", [128, 22, GS], BF16)
        R = P.sb("gr", [128, 8, GS], F32)
        Rb = P.sb("grb", [128, 8, GS], BF16)
        SQ = P.sb("gsq", [128, 8, GS], BF16)
        tmpA = P.sb("gtA", [128, GS], F32)
        tmpB = P.sb("gtB", [128, GS], F32)
        mean_sb = P.sb("gmean", [128, GS], F32)
        m2 = P.sb("gm2", [128, GS], F32)
        rstd = P.sb("grstd", [128, GS], F32)
        groups4b = token_groups(cfg, GS)
        XMb = [XM, P.sb("gxm2", [128, 8, GS], BF16)]

        def load_x(gi):
            t0, n, isctx = groups4b[gi]
            X = Xb[gi % 2]
            for k in range(8):
                K.dma('sp', X.ap[:, k, :n], xres[k * 128:(k + 1) * 128, t0:t0 + n], reads=blks('xres', t0, t0 + n), writes=[X])

        def make_xm(gi):
            t0, n, isctx = groups4b[gi]
            j = 1 if isctx else 0
            X, XMg = Xb[gi % 2], XMb[gi % 2]
            for k in range(8):
                s1, sc_ap = modcol(4, l, k, j)
                s2, sh_ap = modcol(3, l, k, j)
                K.op('act', lambda k=k, sc_ap=sc_ap, sh_ap=sh_ap, X=X, XMg=XMg, n=n: nc.scalar.activation(
                    out=XMg.ap[:, k, :n], in_=X.ap[:, k, :n], func=AF.Identity, scale=sc_ap, bias=sh_ap), reads=[X, s1, s2], writes=[XMg])
        load_x(0)
        make_xm(0)
        for gi, (t0, n, isctx) in enumerate(groups4b):
            b = gi % 2
            j = 1 if isctx else 0
            X = Xb[b]
            XM = XMb[b]
            if gi + 1 < len(groups4b):
                load_x(gi + 1)
            for jj in range(22):
                psg = psr.get()
                K.mm(psg.ap[:, :n], psg, [(wfi.ap[:, k, jj * 128:(jj + 1) * 128], XM.ap[:, k, :n]) for k in range(8)], [wfi, XM])
                psu = psr.get()
                K.mm(psu.ap[:, :n], psu, [(wfi.ap[:, k, D_FF + jj * 128:D_FF + (jj + 1) * 128], XM.ap[:, k, :n]) for k in range(8)], [wfi, XM])
                tb = tmpA if jj % 2 == 0 else tmpB
                K.op('act', lambda psg=psg, tb=tb: nc.scalar.activation(out=tb.ap[:, :n], in_=psg.ap[:, :n], func=AF.Silu), reads=[psg], writes=[tb])
                K.op('dve', lambda psu=psu, tb=tb, jj=jj: nc.vector.tensor_tensor(out=AC.ap[:, jj, :n], in0=tb.ap[:, :n], in1=psu.ap[:, :n], op=ALU.mult),
                     reads=[tb, psu], writes=[AC])
            if gi + 1 < len(groups4b):
                make_xm(gi + 1)
            for m in range(8):
                ps = psr.get()
                K.mm(ps.ap[:, :n], ps, [(wfo.ap[:, k, m * 128:(m + 1) * 128], AC.ap[:, k, :n]) for k in range(22)], [wfo, AC])
                gsrc, gap = modcol(5, l, m, j)
                K.op('dve', lambda ps=ps, m=m, gap=gap, X=X: nc.vector.scalar_tensor_tensor(
                    out=R.ap[:, m, :n], in0=ps.ap[:, :n], scalar=gap, in1=X.ap[:, m, :n], op0=ALU.mult, op1=ALU.add),
                    reads=[ps, gsrc, X], writes=[R])
            layer_norm(P, R, n, 48, 56, X, tmpA, tmpB, Rb, SQ, mean_sb, m2, rstd)
            for k in range(8):
                if last:
                    if not isctx:
                        K.dma('sp', outT[k * 128:(k + 1) * 128, t0 - CTX:t0 - CTX + n], X.ap[:, k, :n], reads=[X])
                    if cfg['debug']:
                        K.dma('sp', xres[k * 128:(k + 1) * 128, t0:t0 + n], X.ap[:, k, :n], reads=[X], writes=blks('xres', t0, t0 + n))
                else:
                    K.dma('sp', xres[k * 128:(k + 1) * 128, t0:t0 + n], X.ap[:, k, :n], reads=[X], writes=blks('xres', t0, t0 + n))


def host_consts(cfg):
    S, T = cfg['S'], cfg['T']
    rows = S // 64
    row = np.repeat(np.arange(rows, dtype=np.float32), 64)
    col = np.tile(np.arange(64, dtype=np.float32), rows)
    nf = 16
    inv = (10000.0 ** (-np.arange(nf, dtype=np.float32) / nf)).astype(np.float32)
    ar = row[:, None] * inv
    ac = col[:, None] * inv
    ang = np.concatenate([ar, ar, ac, ac], axis=-1)
    cos = np.cos(ang).astype(np.float32)
    sin = np.sin(ang).astype(np.float32)
    sign = np.where((np.arange(64) % 32) < 16, -1.0, 1.0).astype(np.float32)
    sin_s = sin * sign[None, :]
    cosT = np.ones((128, T), np.float32)
    sinT = np.zeros((128, T), np.float32)
    cosT[:, CTX:] = np.tile(cos.T, (2, 1))
    sinT[:, CTX:] = np.tile(sin_s.T, (2, 1))
    cmat = np.zeros((128, 8, 128), np.float32)
    cmat[:, 0, :] = np.eye(128)
    ii = np.arange(128)
    cmat[:, 1, :] = (ii[:, None] <= ii[None, :])
    cmat[:, 2, :] = (ii[:, None] >= ii[None, :])
    cmat[:, 3, :] = 1.0 / 1024.0
    blk = np.zeros((128, 128), np.float32)
    blk[:64, :64] = 1.0 / 64
    blk[64:, 64:] = 1.0 / 64
    cmat[:, 4, :] = blk
    cmat[:, 5, :] = 1.0 / 128.0
    cmat[:, 6, :] = 1.0
    csel = np.zeros((128, 16), np.float32)
    csel[0:4, 0] = 1.0
    csel[4:8, 1] = 1.0
    mmask = np.zeros((128, 8, 512), np.float32)
    ss = np.arange(128)[:, None]
    tt = np.arange(512)[None, :]
    for oi in range(4):
        mmask[:, oi, :] = ((oi * 128 + ss) <= tt)
        mmask[:, 4 + oi, :] = ((oi * 128 + ss) >= tt)
    return cosT, sinT, cmat, csel, mmask


_CACHE = {}


def get_program(cfg_key):
    if cfg_key not in _CACHE:
        S, depth = cfg_key
        cfg = cfg_make(S, depth)
        _CACHE[cfg_key] = (build_program(cfg), cfg)
    return _CACHE[cfg_key]


def make_in_maps(cfg, inputs):
    x = np.asarray(inputs['x'], np.float32)
    ctx = np.asarray(inputs['ctx'], np.float32)
    c = np.asarray(inputs['c'], np.float32)
    c_ctx = np.asarray(inputs['c_ctx'], np.float32)
    B = x.shape[0]
    cosT, sinT, cmat, csel, mmask = host_consts(cfg)
    shared = {k: np.ascontiguousarray(np.asarray(inputs[k], np.float32)) for k in (
        'w_ada', 'b_ada', 'w_in', 'mlstm_conv_w', 'mlstm_conv_b', 'mlstm_gate_b', 'mlstm_norm_g', 'diff_lambda',
        'diff_norm_g', 'gqa_q_norm_g', 'gqa_k_norm_g', 'w_out', 'ln1_g', 'ln1_b', 'w_ffn_in', 'w_ffn_out', 'ln2_g', 'ln2_b')}
    Lh = DEPTH_FULL
    pp = np.zeros((Lh, 128, 64), np.float32)
    partner = np.arange(64)
    partner = np.where((partner % 32) < 16, partner + 16, partner - 16)
    for l in range(Lh):
        qg = shared['gqa_q_norm_g'][l]
        kg = shared['gqa_k_norm_g'][l]
        pp[l, :, 0] = np.tile(qg, 2)
        pp[l, :, 1] = np.tile(kg, 2)
        pp[l, :, 2] = np.tile(qg[partner], 2)
        pp[l, :, 3] = np.tile(kg[partner], 2)
        pp[l, :, 4:8] = shared['mlstm_conv_b'][l].reshape(4, 128).T
        pp[l, :, 8:28] = shared['mlstm_conv_w'][l].reshape(5, 4, 128).transpose(2, 1, 0).reshape(128, 20)
        pp[l, :, 28] = shared['diff_norm_g'][l]
        pp[l, :, 29:31] = shared['mlstm_norm_g'][l].reshape(2, 128).T
        pp[l, :, 32:40] = shared['ln1_g'][l].reshape(8, 128).T
        pp[l, :, 40:48] = shared['ln1_b'][l].reshape(8, 128).T
        pp[l, :, 48:56] = shared['ln2_g'][l].reshape(8, 128).T
        pp[l, :, 56:64] = shared['ln2_b'][l].reshape(8, 128).T
    shared.update(cosT=cosT, sinT=sinT, cmat=cmat, csel=csel, pp=pp, mmask=mmask)
    maps = []
    for b in range(B):
        m = dict(shared)
        m['xT_in'] = np.ascontiguousarray(np.concatenate([ctx[b].T, x[b].T], axis=1))
        m['c2T'] = np.ascontiguousarray(np.stack([c[b], c_ctx], axis=1))
        maps.append(m)
    return maps


def kernel(**inputs):
    x = inputs['x']
    B, S, _ = x.shape
    (nc, K), cfg = get_program((S, DEPTH_FULL))
    maps = make_in_maps(cfg, inputs)
    res = run_bass_kernel_spmd(nc, maps, core_ids=list(range(B)))
    out = np.stack([np.ascontiguousarray(r['outT'].T) for r in res.results], axis=0)
    return out.astype(np.float32)
```

```python
import math
from contextlib import ExitStack
import numpy as np
import ml_dtypes
import concourse.bass as bass
import concourse.mybir as mybir
from concourse.bass_utils import run_bass_kernel_spmd

F32 = mybir.dt.float32
BF16 = mybir.dt.bfloat16
AF = mybir.ActivationFunctionType
ALU = mybir.AluOpType
AX = mybir.AxisListType

D = 1024
HD = 64
CTX = 256
DEPTH_FULL = 4
D_FF = 2816
IN_COLS = 3088
LN_EPS = 1e-5
C_MQ, C_MK, C_MV, C_MO, C_MG, C_DQ, C_DK, C_DV, C_GQ, C_GK, C_GV = (
    0, 256, 512, 768, 1024, 1040, 1552, 2064, 2576, 2832, 2960)

STORES_ON_POOL = True
SAME_SYNC = {'pe': False, 'act': True, 'dve': True, 'pool': True, 'sp': False}


class Buf:
    __slots__ = ('name', 'w', 'r', 'ap')

    def __init__(self, name='', ap=None):
        self.name = name
        self.w = {}
        self.r = {}
        self.ap = ap


class Kern:
    def __init__(self, nc, n_dsem_sp=40, n_dsem_pool=40):
        self.nc = nc
        self.E = {'pe': nc.tensor, 'act': nc.scalar, 'dve': nc.vector, 'pool': nc.gpsimd, 'sp': nc.sync}
        self.stack = ExitStack()
        self.sem = {}
        self.cnt = {}
        self.seen = {e: {} for e in self.E}
        for e in self.E:
            self.sem[e] = self.stack.enter_context(nc.semaphore('S_' + e))
            self.cnt[e] = 0
        self.dq = {}
        for q, n in (('sp', n_dsem_sp), ('pool', n_dsem_pool)):
            sems = [self.stack.enter_context(nc.semaphore('D%s%d' % (q, i))) for i in range(n)]
            self.dq[q] = {'sems': sems, 'cnt': [0] * n, 'next': 0}
        self.n_ins = 0
        self.n_wait = 0

    def _semh(self, key):
        if isinstance(key, str):
            return self.sem[key]
        return self.dq[key[0]]['sems'][key[1]]

    def _deps(self, reads, writes, par=False):
        d = {}
        for b in reads:
            for k, v in b.w.items():
                if d.get(k, 0) < v:
                    d[k] = v
        for b in writes:
            for k, v in b.w.items():
                if par and not isinstance(k, str):
                    continue
                if d.get(k, 0) < v:
                    d[k] = v
            for k, v in b.r.items():
                if d.get(k, 0) < v:
                    d[k] = v
        return d

    def _wait(self, e, d):
        eng = self.E[e]
        seen = self.seen[e]
        for k, v in d.items():
            if k == e and not SAME_SYNC[e]:
                continue
            if seen.get(k, 0) >= v:
                continue
            eng.wait_ge(self._semh(k), v)
            seen[k] = v
            self.n_wait += 1

    def _mark(self, tok, reads, writes, par=False):
        k, v = tok
        for b in reads:
            b.r[k] = v
        for b in writes:
            if par:
                b.w = {kk: vv for kk, vv in b.w.items() if not isinstance(kk, str)}
                b.w[k] = v
            else:
                b.w = {k: v}
            b.r = {}

    def op(self, e, fn, reads=(), writes=(), inc=True):
        d = self._deps(reads, writes)
        self._wait(e, d)
        ins = fn()
        self.n_ins += 1
        if inc:
            self.cnt[e] += 1
            ins.then_inc(self.sem[e], 1)
            tok = (e, self.cnt[e])
        else:
            tok = (e, self.cnt[e] + 1)
        self._mark(tok, reads, writes)
        return ins

    def dma(self, q, out, in_, reads=(), writes=(), slow=False, par=True):
        if q == 'sp' and STORES_ON_POOL and type(out.tensor).__name__ == 'DRamTensorHandle' and type(in_.tensor).__name__ != 'DRamTensorHandle':
            q = 'pool'
        d = self._deps(reads, writes, par=par)
        Q = self.dq[q]
        j = Q['next']
        Q['next'] = (j + 1) % len(Q['sems'])
        key = (q, j)
        if Q['cnt'][j] > 0:
            if d.get(key, 0) < Q['cnt'][j]:
                d[key] = Q['cnt'][j]
        self._wait(q, d)
        ins = self.E[q].dma_start(out=out, in_=in_, allow_slow_non_contiguous=True) if slow else self.E[q].dma_start(out=out, in_=in_)
        Q['cnt'][j] += 16
        ins.then_inc(Q['sems'][j], 16)
        self.n_ins += 1
        self._mark((key, Q['cnt'][j]), reads, writes, par=par)
        return ins

    def all_tokens(self):
        d = {}
        for e in self.E:
            if self.cnt[e] > 0:
                d[e] = self.cnt[e]
        for q, Q in self.dq.items():
            for j, c in enumerate(Q['cnt']):
                if c > 0:
                    d[(q, j)] = c
        return d

    def barrier(self, engines=None):
        d = self.all_tokens()
        for e in (engines or self.E):
            dd = {k: v for k, v in d.items() if k != e}
            self._wait(e, dd)

    def mm(self, out_ap, out_buf, pairs, reads):
        n = len(pairs)
        nc = self.nc
        for i, (l, r) in enumerate(pairs):
            self.op('pe', (lambda l=l, r=r, i=i: nc.tensor.matmul(out_ap, lhsT=l, rhs=r, start=(i == 0), stop=(i == n - 1))),
                    reads=reads, writes=[out_buf], inc=(i == n - 1))


def act_recip(K, nc, out_ap, in_ap, reads, out_buf):
    K.op('act', lambda: nc.scalar.activation(out=out_ap, in_=in_ap, func=AF.Ln), reads=reads, writes=[out_buf])
    K.op('act', lambda: nc.scalar.activation(out=out_ap, in_=out_ap, func=AF.Exp, scale=-1.0), reads=[out_buf], writes=[out_buf])


class Phase:
    def __init__(self, K):
        self.K = K
        self.stack = ExitStack()

    def __enter__(self):
        return self

    def __exit__(self, *a):
        self.K.barrier()
        self.stack.close()
        return False

    def sb(self, name, shape, dt):
        self.K.uid = getattr(self.K, 'uid', 0) + 1
        name = "%s_u%d" % (name, self.K.uid)
        t = self.stack.enter_context(self.K.nc.sbuf_tensor(name, list(shape), dt))
        return Buf(name, t)

    def ps(self, name, shape, dt):
        t = self.stack.enter_context(self.K.nc.psum_tensor(name, list(shape), dt))
        return Buf(name, t)


class RR:
    def __init__(self, items):
        self.items = items
        self.i = 0

    def get(self):
        x = self.items[self.i]
        self.i = (self.i + 1) % len(self.items)
        return x

    def peek(self):
        return self.items[self.i]


def cfg_make(S=4096, depth=DEPTH_FULL, debug=False):
    T = CTX + S
    return dict(S=S, T=T, depth=depth, debug=debug, NT=T // 128, NB=T // 256)


def token_groups(cfg, gsz):
    out = [(0, CTX, True)] if gsz >= CTX else [(i, gsz, True) for i in range(0, CTX, gsz)]
    t = CTX
    while t < cfg['T']:
        n = min(gsz, cfg['T'] - t)
        out.append((t, n, False))
        t += n
    return out


def build_program(cfg):
    S, T, depth = cfg['S'], cfg['T'], cfg['depth']
    NT, NB = cfg['NT'], cfg['NB']
    debug = cfg['debug']
    ALPHA = (2 * DEPTH_FULL) ** 0.25
    nc = bass.Bass("TRN2", target_bir_lowering=False)
    K = Kern(nc)
    L = depth

    def dram_in(name, shape, dt=F32):
        return nc.dram_tensor(name, list(shape), dt, kind="ExternalInput").ap()

    skind = "ExternalOutput" if debug else "Internal"

    def dram_scr(name, shape, dt):
        return nc.dram_tensor(name, list(shape), dt, kind=skind).ap()

    xT_in = dram_in("xT_in", [D, T])
    c2T = dram_in("c2T", [D, 2])
    w_ada = dram_in("w_ada", [DEPTH_FULL, D, 6 * D])
    b_ada = dram_in("b_ada", [DEPTH_FULL, 6 * D])
    w_in = dram_in("w_in", [DEPTH_FULL, D, IN_COLS])
    conv_w = dram_in("mlstm_conv_w", [DEPTH_FULL, 5, 512])
    conv_b = dram_in("mlstm_conv_b", [DEPTH_FULL, 512])
    gate_b = dram_in("mlstm_gate_b", [DEPTH_FULL, 16])
    mnorm_g = dram_in("mlstm_norm_g", [DEPTH_FULL, 256])
    dlam = dram_in("diff_lambda", [DEPTH_FULL, 4, 64])
    dnorm_g = dram_in("diff_norm_g", [DEPTH_FULL, 128])
    qn_g = dram_in("gqa_q_norm_g", [DEPTH_FULL, 64])
    kn_g = dram_in("gqa_k_norm_g", [DEPTH_FULL, 64])
    w_out = dram_in("w_out", [DEPTH_FULL, D, D])
    ln1_g = dram_in("ln1_g", [DEPTH_FULL, D])
    ln1_b = dram_in("ln1_b", [DEPTH_FULL, D])
    w_ffn_in = dram_in("w_ffn_in", [DEPTH_FULL, D, 2 * D_FF])
    w_ffn_out = dram_in("w_ffn_out", [DEPTH_FULL, D_FF, D])
    ln2_g = dram_in("ln2_g", [DEPTH_FULL, D])
    ln2_b = dram_in("ln2_b", [DEPTH_FULL, D])
    pp_in = dram_in("pp", [DEPTH_FULL, 128, 64])
    cosT = dram_in("cosT", [128, T])
    sinT = dram_in("sinT", [128, T])
    cmat = dram_in("cmat", [128, 8, 128])
    csel = dram_in("csel", [128, 16])
    outT = nc.dram_tensor("outT", [D, S], F32, kind="ExternalOutput").ap()

    xres = dram_scr("xres", [D, T], F32)
    qkT = dram_scr("qkT", [512, T], F32)
    gT = dram_scr("gT", [16, T], F32)
    mvo = dram_scr("mvt", [T, 256], BF16)
    soT = dram_scr("soT", [256, T], BF16)
    bsc = dram_scr("bsc", [8, T], F32)
    nbrsc = dram_scr("nbrsc", [8 * NT], F32)
    mmask = dram_in("mmask", [128, 8, 512])
    dqkT = dram_scr("dqkT", [1024, T], BF16)
    dvt = dram_scr("dvt", [T, 512], BF16)
    gqkT = dram_scr("gqkT", [384, T], BF16)
    gvt = dram_scr("gvt", [T, 128], BF16)
    yT = dram_scr("yT", [D, T], BF16)
    DB = {n: [Buf('%s_%d' % (n, i)) for i in range(NB)] for n in
          ('xres', 'qkT', 'gT', 'mvo', 'dqkT', 'dvt', 'gqkT', 'gvt', 'yT', 'soT')}
    bsc_buf = Buf('bsc')
    nbrsc_buf = Buf('nbrsc')

    def blks(name, t0, t1):
        t0 = max(t0, 0)
        t1 = min(t1, T)
        return DB[name][t0 // 256:(t1 - 1) // 256 + 1]

    top = ExitStack()

    def psb(name, shape, dt):
        return Buf(name, top.enter_context(nc.sbuf_tensor(name, list(shape), dt)))

    modv = psb("modv", [128, L, 48, 2], F32)
    mod1p = psb("mod1p", [128, L, 48, 2], F32)
    modga = psb("modga", [128, L, 48, 2], F32)
    cm = psb("cm", [128, 8, 128], F32)
    cmb = psb("cmb", [128, 8, 128], BF16)
    cs_ = psb("csel_sb", [128, 16], F32)
    epsc = psb("epsc", [128, 4], F32)
    ppv = psb("ppv", [128, L, 64], F32)
    PS = [Buf('ps%d' % i, top.enter_context(nc.psum_tensor('ps%d' % i, [128, 512], F32))) for i in range(8)]

    ident = cm.ap[:, 0, :]
    identb = cmb.ap[:, 0, :]
    EPS_A = LN_EPS / (ALPHA * ALPHA)

    with Phase(K) as P:
        K.dma('sp', cm.ap[:], cmat, writes=[cm])
        K.op('dve', lambda: nc.vector.memset(epsc.ap[:, 0:1], LN_EPS), writes=[epsc])
        K.op('dve', lambda: nc.vector.memset(epsc.ap[:, 1:2], EPS_A), writes=[epsc])
        K.op('dve', lambda: nc.vector.memset(epsc.ap[:, 2:3], 1.0), writes=[epsc])
        K.dma('sp', cs_.ap[:], csel, writes=[cs_])
        K.op('dve', lambda: nc.vector.tensor_copy(out=cmb.ap[:], in_=cm.ap[:]), reads=[cm], writes=[cmb])
        for l in range(L):
            K.dma('sp', ppv.ap[:, l, :], pp_in[l], writes=[ppv])
        csb = P.sb("c_sb", [128, 8, 2], F32)
        ssb = P.sb("s_sb", [128, 8, 2], F32)
        bada = P.sb("bada", [128, L, 48], F32)
        K.dma('sp', csb.ap[:], c2T.rearrange("(k p) j -> p k j", p=128), writes=[csb], slow=True)
        for l in range(L):
            K.dma('sp', bada.ap[:, l, :], b_ada[l].rearrange("(m p) -> p m", p=128), writes=[bada], slow=True)
        K.op('act', lambda: nc.scalar.activation(out=ssb.ap[:], in_=csb.ap[:], func=AF.Silu), reads=[csb], writes=[ssb])
        wb = [P.sb("wada%d" % i, [128, 8, 1024], F32) for i in range(2)]
        it = 0
        for l in range(L):
            for blk in range(6):
                buf = wb[it % 2]
                it += 1
                for k in range(8):
                    K.dma('sp', buf.ap[:, k, :], w_ada[l, k * 128:(k + 1) * 128, blk * 1024:(blk + 1) * 1024], writes=[buf])
                ps = PS[it % 2]
                for m in range(8):
                    K.mm(ps.ap[:, m * 2:m * 2 + 2], ps,
                         [(buf.ap[:, k, m * 128:(m + 1) * 128], ssb.ap[:, k, :]) for k in range(8)], [buf, ssb])
                K.op('dve', lambda ps=ps, l=l, blk=blk: nc.vector.tensor_tensor(
                    out=modv.ap[:, l, blk * 8:(blk + 1) * 8, :],
                    in0=ps.ap[:, 0:16].rearrange("p (m j) -> p m j", j=2),
                    in1=bada.ap[:, l, blk * 8:(blk + 1) * 8].unsqueeze(2).to_broadcast([128, 8, 2]),
                    op=ALU.add), reads=[ps, bada], writes=[modv])
        K.op('dve', lambda: nc.vector.tensor_scalar(out=mod1p.ap[:], in0=modv.ap[:], scalar1=1.0, scalar2=None, op0=ALU.add),
             reads=[modv], writes=[mod1p])
        K.op('dve', lambda: nc.vector.tensor_scalar(out=modga.ap[:], in0=modv.ap[:], scalar1=1.0 / ALPHA, scalar2=None, op0=ALU.mult),
             reads=[modv], writes=[modga])

    if cfg.get('stop_after') == 'p0':
        dbg = nc.dram_tensor("dbg_mod", [128, L * 96], F32, kind="ExternalOutput").ap()
        K.dma('sp', dbg, modv.ap[:].rearrange("p l m j -> p (l m j)"), reads=[modv])
        K.barrier(engines=['sp'])
        top.close()
        K.stack.close()
        return nc, K

    def modcol(which, l, k, j):
        src = {0: modv, 1: mod1p, 2: modga, 3: modv, 4: mod1p, 5: modga}[which]
        return src, src.ap[:, l, which * 8 + k, j:j + 1]

    for l in range(L):
        xsrc = xT_in if l == 0 else xres
        xsrc_bufs = (lambda t0, t1: []) if l == 0 else (lambda t0, t1: blks('xres', t0, t1))
        with Phase(K) as P:
            w = P.sb("w_in_sb", [128, 8, IN_COLS], BF16)
            wsw = P.sb("w_sw_sb", [128, 8, 1408], BF16)
            for k in range(8):
                K.dma('pool', w.ap[:, k, :], w_in[l, k * 128:(k + 1) * 128, :], writes=[w])
            for (dst0, src0, ncol) in ((0, C_DQ, 1024), (1024, C_GQ, 384)):
                srcv = w.ap[:, :, src0:src0 + ncol].rearrange("p k (g two s) -> p k g two s", two=2, s=16)
                dstv = wsw.ap[:, :, dst0:dst0 + ncol].rearrange("p k (g two s) -> p k g two s", two=2, s=16)
                for k in range(8):
                    K.op('dve', lambda k=k, srcv=srcv, dstv=dstv: nc.vector.tensor_copy(out=dstv[:, k, :, 0, :], in_=srcv[:, k, :, 1, :]),
                         reads=[w], writes=[wsw])
                    K.op('dve', lambda k=k, srcv=srcv, dstv=dstv: nc.vector.tensor_copy(out=dstv[:, k, :, 1, :], in_=srcv[:, k, :, 0, :]),
                         reads=[w], writes=[wsw])
            GS = 512
            xg = [P.sb("xg%d" % i, [128, 8, GS], F32) for i in range(2)]
            xm = [P.sb("xm%d" % i, [128, 8, GS], BF16) for i in range(2)]
            cosb = [P.sb("cos%d" % i, [128, GS], F32) for i in range(2)]
            sinb = [P.sb("sin%d" % i, [128, GS], F32) for i in range(2)]
            st_qk = [P.sb("st_qk%d" % i, [128, 4, GS], F32) for i in range(2)]
            st_g = [P.sb("st_g%d" % i, [16, GS], F32) for i in range(2)]
            st_d = [P.sb("st_d%d" % i, [128, 8, GS], BF16) for i in range(2)]
            st_gq = [P.sb("st_gq%d" % i, [128, 3, GS], BF16) for i in range(2)]
            st_tok = [P.sb("st_tok%d" % i, [128, 896], BF16) for i in range(3)]
            st_so = [P.sb("st_so%d" % i, [128, 2, GS], BF16) for i in range(2)]
            t1b = [P.sb("t1b%d" % i, [128, GS], F32) for i in range(2)]
            t2b = [P.sb("t2b%d" % i, [128, GS], F32) for i in range(2)]
            sqb = [P.sb("sqb%d" % i, [128, GS], BF16) for i in range(2)]
            rb = [P.sb("rb%d" % i, [128, GS], F32) for i in range(2)]
            psr = RR(PS)
            tokr = RR(st_tok)
            tr = RR(list(range(2)))
            for gi, (t0, n, isctx) in enumerate(token_groups(cfg, GS)):
                b = gi % 2
                j = 1 if isctx else 0
                X, XM = xg[b], xm[b]
                for k in range(8):
                    K.dma('sp', X.ap[:, k, :n], xsrc[k * 128:(k + 1) * 128, t0:t0 + n], reads=xsrc_bufs(t0, t0 + n), writes=[X])
                K.dma('sp', cosb[b].ap[:, :n], cosT[:, t0:t0 + n], writes=[cosb[b]])
                K.dma('sp', sinb[b].ap[:, :n], sinT[:, t0:t0 + n], writes=[sinb[b]])
                for k in range(8):
                    sb_, sc_ap = modcol(1, l, k, j)
                    _, sh_ap = modcol(0, l, k, j)
                    K.op('act', lambda k=k, sc_ap=sc_ap, sh_ap=sh_ap, X=X, XM=XM: nc.scalar.activation(
                        out=XM.ap[:, k, :n], in_=X.ap[:, k, :n], func=AF.Identity, scale=sc_ap, bias=sh_ap),
                        reads=[X, mod1p, modv], writes=[XM])

                def fm(col0, m, wt=w):
                    ps = psr.get()
                    K.mm(ps.ap[:m, :n], ps, [(wt.ap[:, k, col0:col0 + m], XM.ap[:, k, :n]) for k in range(8)], [wt, XM])
                    return ps
                parts = cfg.get('parts', 'qk,g,d,gq,tok')
                for mt in range(4 if 'qk' in parts else 0):
                    ps = fm(mt * 128, 128)
                    K.op('act', lambda ps=ps, mt=mt: nc.scalar.copy(out=st_qk[b].ap[:, mt, :n], in_=ps.ap[:, :n]),
                         reads=[ps], writes=[st_qk[b]])
                for mt in range(4 if 'qk' in parts else 0):
                    K.dma('sp', qkT[mt * 128:(mt + 1) * 128, t0:t0 + n], st_qk[b].ap[:, mt, :n], reads=[st_qk[b]], writes=blks('qkT', t0, t0 + n))
                for mt in range(2 if 'qk' in parts else 0):
                    ps = fm(C_MO + mt * 128, 128)
                    K.op('act', lambda ps=ps, mt=mt: nc.scalar.activation(out=st_so[b].ap[:, mt, :n], in_=ps.ap[:, :n], func=AF.Sigmoid),
                         reads=[ps], writes=[st_so[b]])
                for mt in range(2 if 'qk' in parts else 0):
                    K.dma('sp', soT[mt * 128:(mt + 1) * 128, t0:t0 + n], st_so[b].ap[:, mt, :n], reads=[st_so[b]], writes=blks('soT', t0, t0 + n))
                if 'g,' in parts:
                    ps = fm(C_MG, 16)
                    K.op('act', lambda ps=ps: nc.scalar.copy(out=st_g[b].ap[:, :n], in_=ps.ap[:16, :n]), reads=[ps], writes=[st_g[b]])
                    K.dma('sp', gT[:, t0:t0 + n], st_g[b].ap[:, :n], reads=[st_g[b]], writes=blks('gT', t0, t0 + n))
                for mt in range(8 if 'd,' in parts else 0):
                    ps1 = fm(C_DQ + mt * 128, 128)
                    ps2 = fm(mt * 128, 128, wsw)
                    ti = tr.get()
                    K.op('dve', lambda ps1=ps1, ti=ti: nc.vector.tensor_tensor(out=t1b[ti].ap[:, :n], in0=ps1.ap[:, :n], in1=cosb[b].ap[:, :n], op=ALU.mult),
                         reads=[ps1, cosb[b]], writes=[t1b[ti]])
                    K.op('dve', lambda ps2=ps2, ti=ti: nc.vector.tensor_tensor(out=t2b[ti].ap[:, :n], in0=ps2.ap[:, :n], in1=sinb[b].ap[:, :n], op=ALU.mult),
                         reads=[ps2, sinb[b]], writes=[t2b[ti]])
                    K.op('pool', lambda ti=ti, mt=mt: nc.gpsimd.tensor_tensor(out=st_d[b].ap[:, mt, :n], in0=t1b[ti].ap[:, :n], in1=t2b[ti].ap[:, :n], op=ALU.add),
                         reads=[t1b[ti], t2b[ti]], writes=[st_d[b]])
                for mt in range(8 if 'd,' in parts else 0):
                    K.dma('sp', dqkT[mt * 128:(mt + 1) * 128, t0:t0 + n], st_d[b].ap[:, mt, :n], reads=[st_d[b]], writes=blks('dqkT', t0, t0 + n))
                for mt in range(3 if 'gq' in parts else 0):
                    gcol = 0 if mt < 2 else 1
                    ps1 = fm(C_GQ + mt * 128, 128)
                    ps2 = fm(1024 + mt * 128, 128, wsw)
                    ti = tr.get()
                    K.op('act', lambda ps1=ps1, ti=ti: nc.scalar.activation(out=sqb[ti].ap[:, :n], in_=ps1.ap[:, :n], func=AF.Square),
                         reads=[ps1], writes=[sqb[ti]])
                    GQ = cfg.get('gqstep', 9)
                    if GQ < 2:
                        continue
                    ps3 = psr.get()
                    K.mm(ps3.ap[:, :n], ps3, [(cmb.ap[:, 4, :], sqb[ti].ap[:, :n])], [cmb, sqb[ti]])
                    if GQ < 3:
                        continue
                    K.op('act', lambda ps3=ps3, ti=ti: nc.scalar.activation(out=rb[ti].ap[:, :n], in_=ps3.ap[:, :n], func=AF.Ln, bias=epsc.ap[:, 0:1]),
                         reads=[ps3, epsc], writes=[rb[ti]])
                    K.op('act', lambda ti=ti: nc.scalar.activation(out=rb[ti].ap[:, :n], in_=rb[ti].ap[:, :n], func=AF.Exp, scale=-0.5),
                         reads=[rb[ti]], writes=[rb[ti]])
                    if GQ < 4:
                        continue
                    K.op('dve', lambda ps1=ps1, ti=ti, gcol=gcol: nc.vector.scalar_tensor_tensor(
                        out=t1b[ti].ap[:, :n], in0=ps1.ap[:, :n], scalar=ppv.ap[:, l, gcol:gcol + 1], in1=cosb[b].ap[:, :n], op0=ALU.mult, op1=ALU.mult),
                        reads=[ppv, cosb[b]], writes=[t1b[ti], ps1])
                    K.op('dve', lambda ps2=ps2, ti=ti, gcol=gcol: nc.vector.scalar_tensor_tensor(
                        out=t2b[ti].ap[:, :n], in0=ps2.ap[:, :n], scalar=ppv.ap[:, l, 2 + gcol:3 + gcol], in1=sinb[b].ap[:, :n], op0=ALU.mult, op1=ALU.mult),
                        reads=[ps2, ppv, sinb[b]], writes=[t2b[ti]])
                    if GQ < 5:
                        continue
                    K.op('pool', lambda ti=ti: nc.gpsimd.tensor_tensor(out=t1b[ti].ap[:, :n], in0=t1b[ti].ap[:, :n], in1=t2b[ti].ap[:, :n], op=ALU.add),
                         reads=[t1b[ti], t2b[ti]], writes=[t1b[ti]])
                    K.op('dve', lambda ti=ti, mt=mt: nc.vector.tensor_tensor(out=st_gq[b].ap[:, mt, :n], in0=t1b[ti].ap[:, :n], in1=rb[ti].ap[:, :n], op=ALU.mult),
                         reads=[t1b[ti], rb[ti]], writes=[st_gq[b]])
                for mt in range(3 if 'gq' in parts else 0):
                    K.dma('sp', gqkT[mt * 128:(mt + 1) * 128, t0:t0 + n], st_gq[b].ap[:, mt, :n], reads=[st_gq[b]], writes=blks('gqkT', t0, t0 + n))
                for tt in range(n // 128 if 'tok' in parts else 0):
                    ta = t0 + tt * 128
                    stt = tokr.get()
                    for (col0, ncol, dst0) in ((C_MV, 256, 0), (C_DV, 512, 256), (C_GV, 128, 768)):
                        ps = psr.get()
                        K.mm(ps.ap[:, :ncol], ps, [(XM.ap[:, k, tt * 128:(tt + 1) * 128], w.ap[:, k, col0:col0 + ncol]) for k in range(8)], [w, XM])
                        if col0 == C_MV:
                            K.op('act', lambda ps=ps, stt=stt: nc.scalar.copy(out=stt.ap[:, 0:256], in_=ps.ap[:, 0:256]), reads=[ps], writes=[stt])
                        else:
                            K.op('dve', lambda ps=ps, stt=stt, ncol=ncol, dst0=dst0: nc.vector.tensor_copy(out=stt.ap[:, dst0:dst0 + ncol], in_=ps.ap[:, :ncol]), reads=[ps], writes=[stt])
                    K.dma('sp', mvo[ta:ta + 128, :], stt.ap[:, 0:256], reads=[stt], writes=blks('mvo', ta, ta + 128))
                    K.dma('sp', dvt[ta:ta + 128, :], stt.ap[:, 256:768], reads=[stt], writes=blks('dvt', ta, ta + 128))
                    K.dma('sp', gvt[ta:ta + 128, :], stt.ap[:, 768:896], reads=[stt], writes=blks('gvt', ta, ta + 128))
        if cfg.get('stop_after') == 'p1':
            break
        build_attention(K, nc, cfg, l, locals())
        if cfg.get('stop_after') == 'p2':
            break
        build_mlstm(K, nc, cfg, l, locals())
        if cfg.get('stop_after') == 'p3':
            break
        build_ffn(K, nc, cfg, l, locals())

    K.barrier(engines=['sp'])
    top.close()
    K.stack.close()
    return nc, K


def build_attention(K, nc, cfg, l, env):
    T, NT = cfg['T'], cfg['NT']
    PS, cmb, ppv, epsc = env['PS'], env['cmb'], env['ppv'], env['epsc']
    dqkT, dvt, gqkT, gvt, yT, blks, dlam = env['dqkT'], env['dvt'], env['gqkT'], env['gvt'], env['yT'], env['blks'], env['dlam']
    lam_init = 0.8 - 0.6 * math.exp(-0.3 * l)
    if cfg.get('fill_y'):
        for k in range(8):
            K.dma('sp', yT[k * 128:(k + 1) * 128, :], dqkT[k * 128:(k + 1) * 128, :], reads=blks('dqkT', 0, T), writes=blks('yT', 0, T))
        return
    with Phase(K) as P:
        dk = P.sb("dk_sb", [128, 4, T], BF16)
        dv = P.sb("dv_sb", [128, NT, 512], BF16)
        gk = P.sb("gk_sb", [128, T], BF16)
        gv = P.sb("gv_sb", [128, NT, 2, 128], BF16)
        K.op('dve', lambda: nc.vector.memset(gv.ap[:], 1.0), writes=[gv])
        for h in range(4):
            K.dma('sp', dk.ap[:, h, :], dqkT[512 + h * 128:512 + (h + 1) * 128, :], reads=blks('dqkT', 0, T), writes=[dk])
        for kt in range(NT):
            K.dma('sp', dv.ap[:, kt, :], dvt[kt * 128:(kt + 1) * 128, :], reads=blks('dvt', kt * 128, kt * 128 + 128), writes=[dv])
            K.dma('sp', gv.ap[:, kt, 0, 0:64], gvt[kt * 128:(kt + 1) * 128, 0:64], reads=blks('gvt', kt * 128, kt * 128 + 128), writes=[gv])
            K.dma('sp', gv.ap[:, kt, 1, 64:128], gvt[kt * 128:(kt + 1) * 128, 64:128], reads=blks('gvt', kt * 128, kt * 128 + 128), writes=[gv])
        K.dma('sp', gk.ap[:], gqkT[256:384, :], reads=blks('gqkT', 0, T), writes=[gk])
        lv = P.sb("lv", [128, 256], F32)
        lt = P.sb("lt", [128, 256], F32)
        ls = P.sb("ls", [128, 4], F32)
        K.dma('sp', lv.ap[:], dlam[l].rearrange("a d -> (a d)").partition_broadcast(128), writes=[lv])
        K.op('dve', lambda: nc.vector.tensor_tensor(out=lt.ap[:, 0:64], in0=lv.ap[:, 0:64], in1=lv.ap[:, 64:128], op=ALU.mult), reads=[lv], writes=[lt])
        K.op('dve', lambda: nc.vector.tensor_tensor(out=lt.ap[:, 64:128], in0=lv.ap[:, 128:192], in1=lv.ap[:, 192:256], op=ALU.mult), reads=[lv], writes=[lt])
        K.op('dve', lambda: nc.vector.reduce_sum(out=ls.ap[:, 0:1], in_=lt.ap[:, 0:64], axis=AX.X), reads=[lt], writes=[ls])
        K.op('dve', lambda: nc.vector.reduce_sum(out=ls.ap[:, 1:2], in_=lt.ap[:, 64:128], axis=AX.X), reads=[lt], writes=[ls])
        K.op('act', lambda: nc.scalar.activation(out=ls.ap[:, 0:2], in_=ls.ap[:, 0:2], func=AF.Exp), reads=[ls], writes=[ls])
        K.op('dve', lambda: nc.vector.tensor_tensor(out=ls.ap[:, 2:3], in0=ls.ap[:, 1:2], in1=ls.ap[:, 0:1], op=ALU.subtract), reads=[ls], writes=[ls])
        K.op('dve', lambda: nc.vector.tensor_scalar(out=ls.ap[:, 2:3], in0=ls.ap[:, 2:3], scalar1=-lam_init, scalar2=None, op0=ALU.add), reads=[ls], writes=[ls])
        K.op('dve', lambda: nc.vector.tensor_scalar(out=ls.ap[:, 3:4], in0=ppv.ap[:, l, 28:29], scalar1=(1.0 - lam_init), scalar2=None, op0=ALU.mult), reads=[ppv], writes=[ls])
        GS = 512
        dqb = [P.sb("dqb%d" % i, [128, 4, GS], BF16) for i in range(2)]
        gqb = [P.sb("gqb%d" % i, [128, 2, GS], BF16) for i in range(2)]
        NPT = 6
        pt = [P.sb("pt%d" % i, [128, GS], BF16) for i in range(NPT)]
        ptr = RR(pt)
        rsb = P.sb("rsb", [128, GS], F32)
        onb = [P.sb("on%d" % i, [128, GS], F32) for i in range(2)]
        ob = P.sb("ob", [128, GS], F32)
        sqb = P.sb("asq", [128, GS], BF16)
        rrb = P.sb("arr", [128, GS], F32)
        yst = [P.sb("yst%d" % i, [128, GS], BF16) for i in range(2)]
        ystr = RR(yst)
        sbanks = [RR([PS[0], PS[1]]), RR([PS[2], PS[3]])]
        accO2 = [PS[4], PS[5]]
        accS2 = [PS[6], PS[7]]
        ssb = PS[6]
        rs2 = P.sb("rs2", [128, GS], F32)
        ones_b = cmb.ap[:, 6, :]
        ones_f = env['cm'].ap[:, 6, :]
        cm_ = env['cm']
        state = {'it': 0}

        def attn_pair(n, kts, streams):
            nk = len(kts)

            def smm(kt):
                out = []
                for a, st in enumerate(streams):
                    sb_ = sbanks[a].get()
                    K.mm(sb_.ap[:, :n], sb_, [(st['lhs_k'](kt), st['rhs_q'])], st['k_reads'] + st['q_reads'])
                    out.append(sb_)
                return out
            cur = smm(kts[0])
            for i, kt in enumerate(kts):
                nxt = smm(kts[i + 1]) if i + 1 < nk else None
                lastk = (i == nk - 1)
                ps_ = []
                for a, st in enumerate(streams):
                    p = ptr.get()
                    sb_cur = cur[a]
                    K.op('act', lambda sb_cur=sb_cur, p=p: nc.scalar.activation(out=p.ap[:, :n], in_=sb_cur.ap[:, :n], func=AF.Exp, scale=0.125),
                         reads=[sb_cur], writes=[p])
                    ps_.append(p)
                for a, st in enumerate(streams):
                    p = ps_[a]
                    K.op('pe', lambda kt=kt, p=p, i=i, lastk=lastk, st=st: nc.tensor.matmul(st['accO_ap'], lhsT=st['lhs_v'](kt), rhs=p.ap[:, :n], start=(i == 0), stop=lastk),
                         reads=[p] + st['v_reads'], writes=[st['accO_buf']], inc=lastk)
                    if st.get('accS_buf') is not None:
                        K.op('pe', lambda p=p, i=i, lastk=lastk, st=st: nc.tensor.matmul(st['accS_buf'].ap[:, :n], lhsT=ones_b, rhs=p.ap[:, :n], start=(i == 0), stop=lastk),
                             reads=[p, cmb], writes=[st['accS_buf']], inc=lastk)
                cur = nxt

        qblocks = [(0, CTX, True)] + [(t, min(GS, T - t), False) for t in range(CTX, T, GS)]
        for bi, (t0, n, isctx) in enumerate(qblocks):
            b = bi % 2
            kts = list(range(CTX // 128)) if isctx else list(range(NT))
            DQ, GQ = dqb[b], gqb[b]
            for h in range(4):
                K.dma('sp', DQ.ap[:, h, :n], dqkT[h * 128:(h + 1) * 128, t0:t0 + n], reads=blks('dqkT', t0, t0 + n), writes=[DQ])
            for hh in range(4):
                g, jq = hh // 2, hh % 2
                K.dma('sp', GQ.ap[g * 64:(g + 1) * 64, jq, :n], gqkT[hh * 64:(hh + 1) * 64, t0:t0 + n], reads=blks('gqkT', t0, t0 + n), writes=[GQ])
            for h in range(4):
                streams = []
                for m in range(2):
                    base = m * 64
                    streams.append(dict(
                        lhs_k=(lambda kt, h=h, base=base: dk.ap[base:base + 64, h, kt * 128:(kt + 1) * 128]),
                        rhs_q=DQ.ap[base:base + 64, h, :n],
                        lhs_v=(lambda kt, h=h: dv.ap[:, kt, h * 128:(h + 1) * 128]),
                        accO_ap=accO2[m].ap[:, :n], accO_buf=accO2[m],
                        k_reads=[dk], q_reads=[DQ], v_reads=[dv], accS_buf=accS2[m]))
                attn_pair(n, kts, streams)
                for m in range(2):
                    act_recip(K, nc, rsb.ap[:, :n], accS2[m].ap[:, :n], [accS2[m]], rsb)
                    K.op('dve', lambda m=m: nc.vector.tensor_tensor(out=onb[m].ap[:, :n], in0=accO2[m].ap[:, :n], in1=rsb.ap[:, :n], op=ALU.mult),
                         reads=[accO2[m], rsb], writes=[onb[m]])
                K.op('dve', lambda: nc.vector.scalar_tensor_tensor(out=ob.ap[:, :n], in0=onb[1].ap[:, :n], scalar=ls.ap[:, 2:3], in1=onb[0].ap[:, :n], op0=ALU.mult, op1=ALU.add),
                     reads=[onb[0], onb[1], ls], writes=[ob])
                K.op('act', lambda: nc.scalar.activation(out=sqb.ap[:, :n], in_=ob.ap[:, :n], func=AF.Square), reads=[ob], writes=[sqb])
                K.mm(ssb.ap[:, :n], ssb, [(cmb.ap[:, 5, :], sqb.ap[:, :n])], [cmb, sqb])
                K.op('act', lambda: nc.scalar.activation(out=rrb.ap[:, :n], in_=ssb.ap[:, :n], func=AF.Ln, bias=epsc.ap[:, 0:1]), reads=[ssb, epsc], writes=[rrb])
                K.op('act', lambda: nc.scalar.activation(out=rrb.ap[:, :n], in_=rrb.ap[:, :n], func=AF.Exp, scale=-0.5), reads=[rrb], writes=[rrb])
                ys = ystr.get()
                K.op('dve', lambda ys=ys: nc.vector.scalar_tensor_tensor(out=ys.ap[:, :n], in0=ob.ap[:, :n], scalar=ls.ap[:, 3:4], in1=rrb.ap[:, :n], op0=ALU.mult, op1=ALU.mult),
                     reads=[ob, ls, rrb], writes=[ys])
                K.dma('sp', yT[256 + h * 128:256 + (h + 1) * 128, t0:t0 + n], ys.ap[:, :n], reads=[ys], writes=blks('yT', t0, t0 + n))
            for jq in range(2):
                gacc = [PS[4], PS[5]] if jq == 0 else [PS[6], PS[7]]
                streams = []
                for g in range(2):
                    base = g * 64
                    streams.append(dict(
                        lhs_k=(lambda kt, base=base: gk.ap[base:base + 64, kt * 128:(kt + 1) * 128]),
                        rhs_q=GQ.ap[base:base + 64, jq, :n],
                        lhs_v=(lambda kt, g=g: gv.ap[:, kt, g, :]),
                        accO_ap=gacc[g].ap[:, :n], accO_buf=gacc[g],
                        k_reads=[gk], q_reads=[GQ], v_reads=[gv]))
                attn_pair(n, kts, streams)
                K.op('dve', lambda: nc.vector.reciprocal(out=rsb.ap[64:128, :n], in_=gacc[0].ap[64:128, :n]), reads=[gacc[0]], writes=[rsb])
                K.op('dve', lambda: nc.vector.reciprocal(out=rsb.ap[0:64, :n], in_=gacc[1].ap[0:64, :n]), reads=[gacc[1]], writes=[rsb])
                K.dma('sp', rs2.ap[0:64, :n], rsb.ap[64:128, :n], reads=[rsb], writes=[rs2])
                K.dma('sp', rs2.ap[64:128, :n], rsb.ap[0:64, :n], reads=[rsb], writes=[rs2])
                ys = ystr.get()
                K.op('dve', lambda ys=ys: nc.vector.tensor_tensor(out=ys.ap[0:64, :n], in0=gacc[0].ap[0:64, :n], in1=rs2.ap[0:64, :n], op=ALU.mult),
                     reads=[gacc[0], rs2], writes=[ys])
                K.op('dve', lambda ys=ys: nc.vector.tensor_tensor(out=ys.ap[64:128, :n], in0=gacc[1].ap[64:128, :n], in1=rs2.ap[64:128, :n], op=ALU.mult),
                     reads=[gacc[1], rs2], writes=[ys])
                for g in range(2):
                    hh = g * 2 + jq
                    K.dma('sp', yT[768 + hh * 64:768 + (hh + 1) * 64, t0:t0 + n], ys.ap[g * 64:(g + 1) * 64, :n], reads=[ys], writes=blks('yT', t0, t0 + n))


def build_mlstm(K, nc, cfg, l, env):
    T, NT = cfg['T'], cfg['NT']
    PS, cm, cmb, ppv, epsc, cs_ = env['PS'], env['cm'], env['cmb'], env['ppv'], env['epsc'], env['cs_']
    qkT, gT, mvo, soT, yT, blks = env['qkT'], env['gT'], env['mvo'], env['soT'], env['yT'], env['blks']
    bsc, nbrsc, bsc_buf, nbrsc_buf, mmask, gate_b = env['bsc'], env['nbrsc'], env['bsc_buf'], env['nbrsc_buf'], env['mmask'], env['gate_b']
    with Phase(K) as P:
        wkt = P.sb("wkt", [128, NT, 8], F32)
        nbr = P.sb("nbr", [128, 8 * NT], F32)
        with Phase(K) as G:
            GI = G.sb("GI", [8, T], F32)
            GF = G.sb("GF", [8, T], F32)
            CF = G.sb("CF", [8, T], F32)
            BB = G.sb("BB", [8, T], F32)
            ON = G.sb("ON", [8, T], F32)
            gb = G.sb("gb", [8, 4], F32)
            tot = G.sb("tot", [8, 2], F32)
            BR = G.sb("BR", [8, NT], F32)
            K.dma('sp', GI.ap[:], gT[0:8, :], reads=blks('gT', 0, T), writes=[GI])
            K.dma('sp', GF.ap[:], gT[8:16, :], reads=blks('gT', 0, T), writes=[GF])
            K.dma('sp', gb.ap[:, 0:1], gate_b[l, 0:8].rearrange("(p o) -> p o", o=1), writes=[gb], slow=True)
            K.dma('sp', gb.ap[:, 1:2], gate_b[l, 8:16].rearrange("(p o) -> p o", o=1), writes=[gb], slow=True)
            K.op('dve', lambda: nc.vector.tensor_scalar(out=gb.ap[:, 2:3], in0=gb.ap[:, 1:2], scalar1=-1.0, scalar2=None, op0=ALU.mult), reads=[gb], writes=[gb])
            K.op('dve', lambda: nc.vector.memset(ON.ap[:], 1.0), writes=[ON])
            K.op('act', lambda: nc.scalar.activation(out=GF.ap[:], in_=GF.ap[:], func=AF.Exp, scale=-1.0, bias=gb.ap[:, 2:3]), reads=[GF, gb], writes=[GF])
            K.op('act', lambda: nc.scalar.activation(out=GF.ap[:], in_=GF.ap[:], func=AF.Ln, bias=epsc.ap[:8, 2:3]), reads=[GF, epsc], writes=[GF])
            K.op('dve', lambda: nc.vector.tensor_scalar(out=GF.ap[:], in0=GF.ap[:], scalar1=-1.0, scalar2=None, op0=ALU.mult), reads=[GF], writes=[GF])
            K.op('dve', lambda: nc.vector.tensor_scalar(out=GI.ap[:], in0=GI.ap[:], scalar1=gb.ap[:, 0:1], scalar2=None, op0=ALU.add), reads=[GI, gb], writes=[GI])
            K.op('dve', lambda: nc.vector.tensor_tensor_scan(out=CF.ap[:], data0=ON.ap[:], data1=GF.ap[:], initial=0.0, op0=ALU.mult, op1=ALU.add),
                 reads=[ON, GF], writes=[CF])
            K.op('dve', lambda: nc.vector.tensor_copy(out=tot.ap[:, 0:1], in_=CF.ap[:, CTX - 1:CTX]), reads=[CF], writes=[tot])
            K.op('dve', lambda: nc.vector.tensor_tensor(out=tot.ap[:, 1:2], in0=CF.ap[:, CTX - 1:CTX], in1=CF.ap[:, T - 1:T], op=ALU.add), reads=[CF], writes=[tot])
            K.op('dve', lambda: nc.vector.tensor_tensor(out=BB.ap[:], in0=GF.ap[:], in1=CF.ap[:], op=ALU.subtract), reads=[GF, CF], writes=[BB])
            K.op('dve', lambda: nc.vector.tensor_scalar(out=BB.ap[:, :CTX], in0=BB.ap[:, :CTX], scalar1=tot.ap[:, 0:1], scalar2=None, op0=ALU.add), reads=[BB, tot], writes=[BB])
            K.op('dve', lambda: nc.vector.tensor_scalar(out=BB.ap[:, CTX:], in0=BB.ap[:, CTX:], scalar1=tot.ap[:, 1:2], scalar2=None, op0=ALU.add), reads=[BB, tot], writes=[BB])
            K.op('dve', lambda: nc.vector.tensor_scalar(out=BB.ap[:], in0=BB.ap[:], scalar1=cs_.ap[:8, 1:2], scalar2=None, op0=ALU.mult), reads=[BB, cs_], writes=[BB])
            K.op('dve', lambda: nc.vector.scalar_tensor_tensor(out=BB.ap[:], in0=CF.ap[:], scalar=cs_.ap[:8, 0:1], in1=BB.ap[:], op0=ALU.mult, op1=ALU.add),
                 reads=[CF, cs_, BB], writes=[BB])
            Bv = BB.ap[:].rearrange("p (kt s) -> p kt s", s=128)
            K.op('dve', lambda: nc.vector.tensor_scalar(out=BR.ap[:], in0=Bv[:, :, 0], scalar1=cs_.ap[:8, 1:2], scalar2=None, op0=ALU.mult), reads=[BB, cs_], writes=[BR])
            K.op('dve', lambda: nc.vector.scalar_tensor_tensor(out=BR.ap[:], in0=Bv[:, :, 127], scalar=cs_.ap[:8, 0:1], in1=BR.ap[:], op0=ALU.mult, op1=ALU.add),
                 reads=[BB, cs_, BR], writes=[BR])
            K.op('dve', lambda: nc.vector.tensor_tensor(out=GI.ap[:], in0=GI.ap[:], in1=BB.ap[:], op=ALU.subtract), reads=[GI, BB], writes=[GI])
            GIv = GI.ap[:].rearrange("p (kt s) -> p kt s", s=128)
            K.op('dve', lambda: nc.vector.tensor_tensor(out=GIv, in0=GIv, in1=BR.ap[:].unsqueeze(2).to_broadcast([8, NT, 128]), op=ALU.add), reads=[GI, BR], writes=[GI])
            K.op('act', lambda: nc.scalar.activation(out=GI.ap[:], in_=GI.ap[:], func=AF.Exp), reads=[GI], writes=[GI])
            K.op('dve', lambda: nc.vector.tensor_scalar(out=BR.ap[:], in0=BR.ap[:], scalar1=-1.0, scalar2=None, op0=ALU.mult), reads=[BR], writes=[BR])
            K.dma('sp', bsc, BB.ap[:], reads=[BB], writes=[bsc_buf])
            K.dma('sp', nbrsc.rearrange("(h k) -> h k", k=NT), BR.ap[:], reads=[BR], writes=[nbrsc_buf])
            pst = PS[7]
            for kt in range(NT):
                K.op('pe', lambda kt=kt: nc.tensor.transpose(out=pst.ap[:, kt * 8:(kt + 1) * 8], in_=GI.ap[:, kt * 128:(kt + 1) * 128], identity=cm.ap[:8, 0, :8]),
                     reads=[GI, cm], writes=[pst], inc=(kt == NT - 1))
            K.op('dve', lambda: nc.vector.tensor_copy(out=wkt.ap[:].rearrange("p k h -> p (k h)"), in_=pst.ap[:, :NT * 8]), reads=[pst], writes=[wkt])
            K.dma('sp', nbr.ap[:], nbrsc.partition_broadcast(128), reads=[nbrsc_buf], writes=[nbr])
        qc = P.sb("qc", [128, 2, T], BF16)
        kc = P.sb("kc", [128, 2, T], BF16)
        ub = [P.sb("ub%d" % i, [128, 4, 516], F32) for i in range(2)]
        accb = [P.sb("cacc%d" % i, [128, 512], F32) for i in range(2)]
        pieces = [(0, CTX, 0, CTX)] + [(t, min(512, T - t), CTX, T) for t in range(CTX, T, 512)]
        for pi, (t0, n, s0, s1) in enumerate(pieces):
            U = ub[pi % 2]
            lo, hi = max(s0, t0 - 2), min(s1, t0 + n + 2)
            K.op('pool', lambda U=U: nc.gpsimd.memset(U.ap[:], 0.0), writes=[U])
            for mt in range(4):
                K.dma('sp', U.ap[:, mt, lo - (t0 - 2):hi - (t0 - 2)], qkT[mt * 128:(mt + 1) * 128, lo:hi], reads=blks('qkT', lo, hi), writes=[U])
            for mt in range(4):
                A = accb[mt % 2]
                wcol = 8 + mt * 5
                K.op('act', lambda U=U, A=A, mt=mt, wcol=wcol: nc.scalar.activation(out=A.ap[:, :n], in_=U.ap[:, mt, 0:n], func=AF.Identity, scale=ppv.ap[:, l, wcol:wcol + 1]),
                     reads=[U, ppv], writes=[A])
                for jj in range(1, 5):
                    K.op('dve', lambda U=U, A=A, mt=mt, jj=jj, wcol=wcol: nc.vector.scalar_tensor_tensor(
                        out=A.ap[:, :n], in0=U.ap[:, mt, jj:jj + n], scalar=ppv.ap[:, l, wcol + jj:wcol + jj + 1], in1=A.ap[:, :n], op0=ALU.mult, op1=ALU.add),
                        reads=[U, ppv, A], writes=[A])
                if mt < 2:
                    K.op('act', lambda A=A, mt=mt: nc.scalar.activation(out=A.ap[:, :n], in_=A.ap[:, :n], func=AF.Silu, bias=ppv.ap[:, l, 4 + mt:5 + mt]), reads=[A, ppv], writes=[A])
                    K.op('dve', lambda A=A, mt=mt: nc.vector.tensor_scalar(out=qc.ap[:, mt, t0:t0 + n], in0=A.ap[:, :n], scalar1=0.125, scalar2=None, op0=ALU.mult), reads=[A], writes=[qc])
                else:
                    K.op('act', lambda A=A, mt=mt: nc.scalar.activation(out=kc.ap[:, mt - 2, t0:t0 + n], in_=A.ap[:, :n], func=AF.Silu, bias=ppv.ap[:, l, 4 + mt:5 + mt]),
                         reads=[A, ppv], writes=[kc])
        mv = P.sb("mv_sb", [128, NT, 4, 128], BF16)
        K.op('dve', lambda: nc.vector.memset(mv.ap[:], 1.0), writes=[mv])
        for kt in range(NT):
            for h in range(4):
                c0 = (h % 2) * 64
                K.dma('sp', mv.ap[:, kt, h, c0:c0 + 64], mvo[kt * 128:(kt + 1) * 128, h * 64:(h + 1) * 64], reads=blks('mvo', kt * 128, kt * 128 + 128), writes=[mv])
        mk = P.sb("mmask_sb", [128, 8, 512], BF16)
        K.dma('pool', mk.ap[:], mmask, writes=[mk])
        GS = 512
        NPT = 6
        pt = [P.sb("mpt%d" % i, [128, GS], BF16) for i in range(NPT)]
        ptr = RR(pt)
        pmb = [P.sb("mpm%d" % i, [128, GS], BF16) for i in range(2)]
        pmr = RR(pmb)
        etb = [P.sb("met%d" % i, [128, GS], F32) for i in range(4)]
        etr = RR(etb)
        bqb = [P.sb("mbq%d" % i, [128, GS], F32) for i in range(3)]
        bqr = RR(bqb)
        sob = [P.sb("mso%d" % i, [128, GS], BF16) for i in range(2)]
        hT = [P.sb("mhT%d" % i, [128, GS], F32) for i in range(2)]
        ddb = P.sb("mdd", [128, GS], F32)
        dd2 = P.sb("mdd2", [128, GS], F32)
        hs = P.sb("mhs", [128, GS], F32)
        hbb = P.sb("mhb", [128, GS], BF16)
        hsq = P.sb("mhsq", [128, GS], BF16)
        mean_sb = P.sb("mmean", [128, GS], F32)
        m2 = P.sb("mm2", [128, GS], F32)
        rstd = P.sb("mrstd", [128, GS], F32)
        yst = [P.sb("myst%d" % i, [128, GS], BF16) for i in range(2)]
        sbank = RR(PS[0:3])
        accr = RR([(PS[3], PS[4]), (PS[5], PS[6])])
        ones = cmb.ap[:, 6, :]
        qblocks = [(0, CTX, True)] + [(t, min(GS, T - t), False) for t in range(CTX, T, GS)]

        def key_tiles(d, t0, n, isctx):
            out = []
            if isctx:
                return [(kt, kt * 128) for kt in range(CTX // 128)]
            if d == 0:
                for kt in range(NT):
                    if (kt + 1) * 128 <= t0:
                        out.append((kt, None))
                    elif kt * 128 < t0 + n:
                        out.append((kt, kt * 128 - t0))
            else:
                for kt in range(NT):
                    if kt < CTX // 128:
                        out.append((kt, None))
                    elif kt * 128 >= t0 + n:
                        out.append((kt, None))
                    elif (kt + 1) * 128 > t0:
                        out.append((kt, kt * 128 - t0))
            return out

        msb = [RR([PS[0], PS[1]]), RR([PS[2], PS[3]])]
        maccr = RR([(PS[4], PS[5]), (PS[6], PS[7])])
        ones_f = cm.ap[:, 6, :]
        bq_sets = [[P.sb("mbqs%d_%d" % (a, c), [128, GS], F32) for c in range(4)] for a in range(2)]
        iters = [(j, t0, n, isctx) for j in range(2) for (t0, n, isctx) in qblocks]

        def load_bq(ii):
            j_, t0_, n_, _ = iters[ii]
            for d_ in range(2):
                for hp_ in range(2):
                    hd_ = d_ * 4 + 2 * j_ + hp_
                    BQ_ = bq_sets[ii % 2][d_ * 2 + hp_]
                    K.dma('sp', BQ_.ap[:, :n_], bsc[hd_, t0_:t0_ + n_].partition_broadcast(128), reads=[bsc_buf], writes=[BQ_])
        deferred = []

        def run_deferred(i, force):
            for item in list(deferred):
                if force or i >= item[0]:
                    deferred.remove(item)
                    item[1]()
        load_bq(0)
        for ii, (j, t0, n, isctx) in enumerate(iters):
            bi = ii + 1
            if True:
                if ii + 1 < len(iters):
                    load_bq(ii + 1)
                SO = sob[bi % 2]
                K.dma('sp', SO.ap[:, :n], soT[j * 128:(j + 1) * 128, t0:t0 + n], reads=blks('soT', t0, t0 + n), writes=[SO])
                for d in range(2):
                    accN, accD = maccr.get()
                    kts = key_tiles(d, t0, n, isctx)
                    nk = len(kts)
                    BQs = [bq_sets[ii % 2][d * 2 + hp] for hp in range(2)]

                    def smm(kt):
                        out = []
                        for hp in range(2):
                            base = hp * 64
                            sb_ = msb[hp].get()
                            K.mm(sb_.ap[:, :n], sb_, [(kc.ap[base:base + 64, j, kt * 128:(kt + 1) * 128], qc.ap[base:base + 64, j, t0:t0 + n])], [kc, qc])
                            out.append(sb_)
                        return out
                    cur = smm(kts[0][0])
                    for i, (kt, off) in enumerate(kts):
                        lastk = (i == nk - 1)
                        run_deferred(i, lastk)
                        nxt = smm(kts[i + 1][0]) if i + 1 < nk else None
                        for hp in range(2):
                            h = 2 * j + hp
                            hd = d * 4 + h
                            BQ = BQs[hp]
                            sb_cur = cur[hp]
                            et = etr.get()
                            if off is None:
                                K.op('act', lambda et=et, BQ=BQ, kt=kt, hd=hd: nc.scalar.activation(out=et.ap[:, :n], in_=BQ.ap[:, :n], func=AF.Exp, bias=nbr.ap[:, hd * NT + kt:hd * NT + kt + 1]),
                                     reads=[BQ, nbr], writes=[et])
                            else:
                                K.op('dve', lambda et=et, BQ=BQ, kt=kt, hd=hd: nc.vector.tensor_scalar(out=et.ap[:, :n], in0=BQ.ap[:, :n], scalar1=nbr.ap[:, hd * NT + kt:hd * NT + kt + 1],
                                                                                                    scalar2=60.0, op0=ALU.add, op1=ALU.min), reads=[BQ, nbr], writes=[et])
                                K.op('act', lambda et=et: nc.scalar.activation(out=et.ap[:, :n], in_=et.ap[:, :n], func=AF.Exp), reads=[et], writes=[et])
                            p = ptr.get()
                            if off is None:
                                K.op('dve', lambda sb_cur=sb_cur, et=et, p=p, kt=kt, hd=hd: nc.vector.scalar_tensor_tensor(
                                    out=p.ap[:, :n], in0=sb_cur.ap[:, :n], scalar=wkt.ap[:, kt, hd:hd + 1], in1=et.ap[:, :n], op0=ALU.mult, op1=ALU.mult),
                                    reads=[sb_cur, wkt, et], writes=[p])
                            else:
                                pm_ = pmr.get()
                                K.op('dve', lambda sb_cur=sb_cur, et=et, pm_=pm_, kt=kt, hd=hd: nc.vector.scalar_tensor_tensor(
                                    out=pm_.ap[:, :n], in0=sb_cur.ap[:, :n], scalar=wkt.ap[:, kt, hd:hd + 1], in1=et.ap[:, :n], op0=ALU.mult, op1=ALU.mult),
                                    reads=[sb_cur, wkt, et], writes=[pm_])
                                mi = d * 4 + off // 128
                                K.op('dve', lambda pm_=pm_, p=p, mi=mi: nc.vector.tensor_tensor(out=p.ap[:, :n], in0=pm_.ap[:, :n], in1=mk.ap[:, mi, :n], op=ALU.mult),
                                     reads=[pm_, mk], writes=[p])
                            acc_hp = accN if hp == 0 else accD
                            K.op('pe', lambda kt=kt, p=p, i=i, lastk=lastk, h=h, acc_hp=acc_hp: nc.tensor.matmul(acc_hp.ap[:, :n], lhsT=mv.ap[:, kt, h, :], rhs=p.ap[:, :n], start=(i == 0), stop=lastk),
                                 reads=[p, mv], writes=[acc_hp], inc=lastk)
                        cur = nxt

                    def tail(accN=accN, accD=accD, d=d, n=n):
                        for (bk, r0) in ((accN, 64), (accD, 0)):
                            K.op('dve', lambda bk=bk, r0=r0: nc.vector.tensor_scalar(out=ddb.ap[r0:r0 + 64, :n], in0=bk.ap[r0:r0 + 64, :n], scalar1=-1.0, scalar2=1.0, op0=ALU.mult, op1=ALU.max),
                                 reads=[bk], writes=[ddb])
                            K.op('dve', lambda bk=bk, r0=r0: nc.vector.tensor_tensor(out=ddb.ap[r0:r0 + 64, :n], in0=bk.ap[r0:r0 + 64, :n], in1=ddb.ap[r0:r0 + 64, :n], op=ALU.max),
                                 reads=[bk, ddb], writes=[ddb])
                        act_recip(K, nc, ddb.ap[:, :n], ddb.ap[:, :n], [ddb], ddb)
                        K.dma('sp', dd2.ap[0:64, :n], ddb.ap[64:128, :n], reads=[ddb], writes=[dd2])
                        K.dma('sp', dd2.ap[64:128, :n], ddb.ap[0:64, :n], reads=[ddb], writes=[dd2])

                    def tail2(accN=accN, accD=accD, d=d, n=n):
                        K.op('dve', lambda: nc.vector.tensor_tensor(out=hT[d].ap[0:64, :n], in0=accN.ap[0:64, :n], in1=dd2.ap[0:64, :n], op=ALU.mult),
                             reads=[accN, dd2], writes=[hT[d]])
                        K.op('dve', lambda: nc.vector.tensor_tensor(out=hT[d].ap[64:128, :n], in0=accD.ap[64:128, :n], in1=dd2.ap[64:128, :n], op=ALU.mult),
                             reads=[accD, dd2], writes=[hT[d]])
                    deferred.append([1, tail])
                    deferred.append([4, tail2])

                def ln_block(j=j, t0=t0, n=n, SO=SO, bi=bi):
                    K.op('dve', lambda: nc.vector.tensor_tensor(out=hs.ap[:, :n], in0=hT[0].ap[:, :n], in1=hT[1].ap[:, :n], op=ALU.add), reads=[hT[0], hT[1]], writes=[hs])
                    K.op('act', lambda: nc.scalar.copy(out=hbb.ap[:, :n], in_=hs.ap[:, :n]), reads=[hs], writes=[hbb])
                    K.op('act', lambda: nc.scalar.activation(out=hsq.ap[:, :n], in_=hs.ap[:, :n], func=AF.Square), reads=[hs], writes=[hsq])
                    pm = msb[0].peek()
                    K.mm(pm.ap[:, :n], pm, [(cmb.ap[:, 4, :], hbb.ap[:, :n])], [cmb, hbb])
                    pq = msb[1].peek()
                    K.mm(pq.ap[:, :n], pq, [(cmb.ap[:, 4, :], hsq.ap[:, :n])], [cmb, hsq])
                    K.op('act', lambda: nc.scalar.copy(out=mean_sb.ap[:, :n], in_=pm.ap[:, :n]), reads=[pm], writes=[mean_sb])
                    K.op('dve', lambda: nc.vector.tensor_tensor(out=m2.ap[:, :n], in0=mean_sb.ap[:, :n], in1=mean_sb.ap[:, :n], op=ALU.mult), reads=[mean_sb], writes=[m2])
                    K.op('dve', lambda: nc.vector.tensor_tensor(out=m2.ap[:, :n], in0=pq.ap[:, :n], in1=m2.ap[:, :n], op=ALU.subtract), reads=[pq, m2], writes=[m2])
                    K.op('act', lambda: nc.scalar.activation(out=rstd.ap[:, :n], in_=m2.ap[:, :n], func=AF.Ln, bias=epsc.ap[:, 0:1]), reads=[m2, epsc], writes=[rstd])
                    K.op('act', lambda: nc.scalar.activation(out=rstd.ap[:, :n], in_=rstd.ap[:, :n], func=AF.Exp, scale=-0.5), reads=[rstd], writes=[rstd])
                    K.op('dve', lambda: nc.vector.tensor_tensor(out=hs.ap[:, :n], in0=hs.ap[:, :n], in1=mean_sb.ap[:, :n], op=ALU.subtract), reads=[hs, mean_sb], writes=[hs])
                    K.op('dve', lambda: nc.vector.tensor_tensor(out=hs.ap[:, :n], in0=hs.ap[:, :n], in1=rstd.ap[:, :n], op=ALU.mult), reads=[hs, rstd], writes=[hs])
                    ys = yst[bi % 2]
                    K.op('dve', lambda: nc.vector.scalar_tensor_tensor(out=ys.ap[:, :n], in0=hs.ap[:, :n], scalar=ppv.ap[:, l, 29 + j:30 + j], in1=SO.ap[:, :n], op0=ALU.mult, op1=ALU.mult),
                         reads=[hs, ppv, SO], writes=[ys])
                    K.dma('sp', yT[j * 128:(j + 1) * 128, t0:t0 + n], ys.ap[:, :n], reads=[ys], writes=blks('yT', t0, t0 + n))
                deferred.append([7, ln_block])
        run_deferred(0, True)


def build_ffn(K, nc, cfg, l, env):
    PS, cmb, ppv, epsc, modcol = env['PS'], env['cmb'], env['ppv'], env['epsc'], env['modcol']
    modv, mod1p, modga = env['modv'], env['mod1p'], env['modga']
    xres, yT, outT, blks = env['xres'], env['yT'], env['outT'], env['blks']
    w_out, w_ffn_in, w_ffn_out, xT_in = env['w_out'], env['w_ffn_in'], env['w_ffn_out'], env['xT_in']
    L = cfg['depth']
    last = (l == L - 1)
    xsrc = xT_in if l == 0 else xres
    xsrc_bufs = (lambda t0, t1: []) if l == 0 else (lambda t0, t1: blks('xres', t0, t1))
    psr = RR(PS)

    def layer_norm(P, R, n, gcol0, bcol0, OUT, tmpA, tmpB, Rb, SQ, mean_sb, m2, rstd, post=None):
        K.op('act', lambda: nc.scalar.copy(out=Rb.ap[:, :, :n], in_=R.ap[:, :, :n]), reads=[R], writes=[Rb])
        K.op('act', lambda: nc.scalar.activation(out=SQ.ap[:, :, :n], in_=R.ap[:, :, :n], func=AF.Square), reads=[R], writes=[SQ])
        pm = psr.get()
        K.mm(pm.ap[:, :n], pm, [(cmb.ap[:, 3, :], Rb.ap[:, m, :n]) for m in range(8)], [cmb, Rb])
        pq = psr.get()
        K.mm(pq.ap[:, :n], pq, [(cmb.ap[:, 3, :], SQ.ap[:, m, :n]) for m in range(8)], [cmb, SQ])
        K.op('act', lambda: nc.scalar.copy(out=mean_sb.ap[:, :n], in_=pm.ap[:, :n]), reads=[pm], writes=[mean_sb])
        K.op('dve', lambda: nc.vector.tensor_tensor(out=m2.ap[:, :n], in0=mean_sb.ap[:, :n], in1=mean_sb.ap[:, :n], op=ALU.mult), reads=[mean_sb], writes=[m2])
        K.op('dve', lambda: nc.vector.tensor_tensor(out=m2.ap[:, :n], in0=pq.ap[:, :n], in1=m2.ap[:, :n], op=ALU.subtract), reads=[pq, m2], writes=[m2])
        K.op('act', lambda: nc.scalar.activation(out=rstd.ap[:, :n], in_=m2.ap[:, :n], func=AF.Ln, bias=epsc.ap[:, 1:2]), reads=[m2, epsc], writes=[rstd])
        K.op('act', lambda: nc.scalar.activation(out=rstd.ap[:, :n], in_=rstd.ap[:, :n], func=AF.Exp, scale=-0.5), reads=[rstd], writes=[rstd])
        K.op('dve', lambda: nc.vector.tensor_tensor(out=R.ap[:, :, :n], in0=R.ap[:, :, :n], in1=mean_sb.ap[:, :n].unsqueeze(1).to_broadcast([128, 8, n]), op=ALU.subtract),
             reads=[R, mean_sb], writes=[R])
        K.op('dve', lambda: nc.vector.tensor_tensor(out=R.ap[:, :, :n], in0=R.ap[:, :, :n], in1=rstd.ap[:, :n].unsqueeze(1).to_broadcast([128, 8, n]), op=ALU.mult),
             reads=[R, rstd], writes=[R])
        for m in range(8):
            K.op('act', lambda m=m: nc.scalar.activation(out=OUT.ap[:, m, :n], in_=R.ap[:, m, :n], func=AF.Identity,
                                                        scale=ppv.ap[:, l, gcol0 + m:gcol0 + m + 1], bias=ppv.ap[:, l, bcol0 + m:bcol0 + m + 1]),
                 reads=[R, ppv], writes=[OUT])
            if post is not None:
                post(m)

    with Phase(K) as P:
        wo = P.sb("wo_sb", [128, 8, 1024], BF16)
        for k in range(8):
            K.dma('pool', wo.ap[:, k, :], w_out[l, k * 128:(k + 1) * 128, :], writes=[wo])
        GS = 512
        Xb = [P.sb("fx%d" % i, [128, 8, GS], F32) for i in range(2)]
        Yb = [P.sb("fy%d" % i, [128, 8, GS], BF16) for i in range(2)]
        Rr = [P.sb("fr%d" % i, [128, 8, GS], F32) for i in range(2)]
        Rb = P.sb("frb", [128, 8, GS], BF16)
        SQ = P.sb("fsq", [128, 8, GS], BF16)
        tmpA = P.sb("ftA", [128, GS], F32)
        tmpB = P.sb("ftB", [128, GS], F32)
        mean_sb = P.sb("fmean", [128, GS], F32)
        m2 = P.sb("fm2", [128, GS], F32)
        rstd = P.sb("frstd", [128, GS], F32)
        for gi, (t0, n, isctx) in enumerate(token_groups(cfg, GS)):
            b = gi % 2
            j = 1 if isctx else 0
            X, Y = Xb[b], Yb[b]
            R = Rr[b]
            for k in range(8):
                K.dma('sp', X.ap[:, k, :n], xsrc[k * 128:(k + 1) * 128, t0:t0 + n], reads=xsrc_bufs(t0, t0 + n), writes=[X])
                K.dma('sp', Y.ap[:, k, :n], yT[k * 128:(k + 1) * 128, t0:t0 + n], reads=blks('yT', t0, t0 + n), writes=[Y])
            for m in range(8):
                ps = psr.get()
                K.mm(ps.ap[:, :n], ps, [(wo.ap[:, k, m * 128:(m + 1) * 128], Y.ap[:, k, :n]) for k in range(8)], [wo, Y])
                gsrc, gap = modcol(2, l, m, j)
                K.op('dve', lambda ps=ps, m=m, gap=gap, X=X: nc.vector.scalar_tensor_tensor(
                    out=R.ap[:, m, :n], in0=ps.ap[:, :n], scalar=gap, in1=X.ap[:, m, :n], op0=ALU.mult, op1=ALU.add),
                    reads=[ps, gsrc, X], writes=[R])
            layer_norm(P, R, n, 32, 40, X, tmpA, tmpB, Rb, SQ, mean_sb, m2, rstd)
            for k in range(8):
                K.dma('sp', xres[k * 128:(k + 1) * 128, t0:t0 + n], X.ap[:, k, :n], reads=[X], writes=blks('xres', t0, t0 + n))

    with Phase(K) as P:
        wfi = P.sb("wfi_sb", [128, 8, 2 * D_FF], BF16)
        wfo = P.sb("wfo_sb", [128, 22, 1024], BF16)
        for k in range(8):
            K.dma('pool', wfi.ap[:, k, :], w_ffn_in[l, k * 128:(k + 1) * 128, :], writes=[wfi])
        for k in range(22):
            K.dma('pool', wfo.ap[:, k, :], w_ffn_out[l, k * 128:(k + 1) * 128, :], writes=[wfo])
        GS = 256
        Xb = [P.sb("gx%d" % i, [128, 8, GS], F32) for i in range(2)]
        XM = P.sb("gxm", [128, 8, GS], BF16)
        AC = P.sb("gac", [128, 22, GS], BF16)
        R = P.sb("gr", [128, 8, GS], F32)
        Rb = P.sb("grb", [128, 8, GS], BF16)
        SQ = P.sb("gsq", [128, 8, GS], BF16)
        tmpA = P.sb("gtA", [128, GS], F32)
        tmpB = P.sb("gtB", [128, GS], F32)
        mean_sb = P.sb("gmean", [128, GS], F32)
        m2 = P.sb("gm2", [128, GS], F32)
        rstd = P.sb("grstd", [128, GS], F32)
        groups4b = token_groups(cfg, GS)
        XMb = [XM, P.sb("gxm2", [128, 8, GS], BF16)]

        def load_x(gi):
            t0, n, isctx = groups4b[gi]
            X = Xb[gi % 2]
            for k in range(8):
                K.dma('sp', X.ap[:, k, :n], xres[k * 128:(k + 1) * 128, t0:t0 + n], reads=blks('xres', t0, t0 + n), writes=[X])

        def make_xm(gi):
            t0, n, isctx = groups4b[gi]
            j = 1 if isctx else 0
            X, XMg = Xb[gi % 2], XMb[gi % 2]
            for k in range(8):
                s1, sc_ap = modcol(4, l, k, j)
                s2, sh_ap = modcol(3, l, k, j)
                K.op('act', lambda k=k, sc_ap=sc_ap, sh_ap=sh_ap, X=X, XMg=XMg, n=n: nc.scalar.activation(
                    out=XMg.ap[:, k, :n], in_=X.ap[:, k, :n], func=AF.Identity, scale=sc_ap, bias=sh_ap), reads=[X, s1, s2], writes=[XMg])
        load_x(0)
        make_xm(0)
        for gi, (t0, n, isctx) in enumerate(groups4b):
            b = gi % 2
            j = 1 if isctx else 0
            X = Xb[b]
            XM = XMb[b]
            if gi + 1 < len(groups4b):
                load_x(gi + 1)
            for jj in range(22):
                psg = psr.get()
                K.mm(psg.ap[:, :n], psg, [(wfi.ap[:, k, jj * 128:(jj + 1) * 128], XM.ap[:, k, :n]) for k in range(8)], [wfi, XM])
                psu = psr.get()
                K.mm(psu.ap[:, :n], psu, [(wfi.ap[:, k, D_FF + jj * 128:D_FF + (jj + 1) * 128], XM.ap[:, k, :n]) for k in range(8)], [wfi, XM])
                tb = tmpA if jj % 2 == 0 else tmpB
                K.op('act', lambda psg=psg, tb=tb: nc.scalar.activation(out=tb.ap[:, :n], in_=psg.ap[:, :n], func=AF.Silu), reads=[psg], writes=[tb])
                K.op('dve', lambda psu=psu, tb=tb, jj=jj: nc.vector.tensor_tensor(out=AC.ap[:, jj, :n], in0=tb.ap[:, :n], in1=psu.ap[:, :n], op=ALU.mult),
                     reads=[tb, psu], writes=[AC])
            if gi + 1 < len(groups4b):
                make_xm(gi + 1)
            for m in range(8):
                ps = psr.get()
                K.mm(ps.ap[:, :n], ps, [(wfo.ap[:, k, m * 128:(m + 1) * 128], AC.ap[:, k, :n]) for k in range(22)], [wfo, AC])
                gsrc, gap = modcol(5, l, m, j)
                K.op('dve', lambda ps=ps, m=m, gap=gap, X=X: nc.vector.scalar_tensor_tensor(
                    out=R.ap[:, m, :n], in0=ps.ap[:, :n], scalar=gap, in1=X.ap[:, m, :n], op0=ALU.mult, op1=ALU.add),
                    reads=[ps, gsrc, X], writes=[R])
            layer_norm(P, R, n, 48, 56, X, tmpA, tmpB, Rb, SQ, mean_sb, m2, rstd)
            for k in range(8):
                if last:
                    if not isctx:
                        K.dma('sp', outT[k * 128:(k + 1) * 128, t0 - CTX:t0 - CTX + n], X.ap[:, k, :n], reads=[X])
                    if cfg['debug']:
                        K.dma('sp', xres[k * 128:(k + 1) * 128, t0:t0 + n], X.ap[:, k, :n], reads=[X], writes=blks('xres', t0, t0 + n))
                else:
                    K.dma('sp', xres[k * 128:(k + 1) * 128, t0:t0 + n], X.ap[:, k, :n], reads=[X], writes=blks('xres', t0, t0 + n))


def host_consts(cfg):
    S, T = cfg['S'], cfg['T']
    rows = S // 64
    row = np.repeat(np.arange(rows, dtype=np.float32), 64)
    col = np.tile(np.arange(64, dtype=np.float32), rows)
    nf = 16
    inv = (10000.0 ** (-np.arange(nf, dtype=np.float32) / nf)).astype(np.float32)
    ar = row[:, None] * inv
    ac = col[:, None] * inv
    ang = np.concatenate([ar, ar, ac, ac], axis=-1)
    cos = np.cos(ang).astype(np.float32)
    sin = np.sin(ang).astype(np.float32)
    sign = np.where((np.arange(64) % 32) < 16, -1.0, 1.0).astype(np.float32)
    sin_s = sin * sign[None, :]
    cosT = np.ones((128, T), np.float32)
    sinT = np.zeros((128, T), np.float32)
    cosT[:, CTX:] = np.tile(cos.T, (2, 1))
    sinT[:, CTX:] = np.tile(sin_s.T, (2, 1))
    cmat = np.zeros((128, 8, 128), np.float32)
    cmat[:, 0, :] = np.eye(128)
    ii = np.arange(128)
    cmat[:, 1, :] = (ii[:, None] <= ii[None, :])
    cmat[:, 2, :] = (ii[:, None] >= ii[None, :])
    cmat[:, 3, :] = 1.0 / 1024.0
    blk = np.zeros((128, 128), np.float32)
    blk[:64, :64] = 1.0 / 64
    blk[64:, 64:] = 1.0 / 64
    cmat[:, 4, :] = blk
    cmat[:, 5, :] = 1.0 / 128.0
    cmat[:, 6, :] = 1.0
    csel = np.zeros((128, 16), np.float32)
    csel[0:4, 0] = 1.0
    csel[4:8, 1] = 1.0
    mmask = np.zeros((128, 8, 512), np.float32)
    ss = np.arange(128)[:, None]
    tt = np.arange(512)[None, :]
    for oi in range(4):
        mmask[:, oi, :] = ((oi * 128 + ss) <= tt)
        mmask[:, 4 + oi, :] = ((oi * 128 + ss) >= tt)
    return cosT, sinT, cmat, csel, mmask


_CACHE = {}


def get_program(cfg_key):
    if cfg_key not in _CACHE:
        S, depth = cfg_key
        cfg = cfg_make(S, depth)
        _CACHE[cfg_key] = (build_program(cfg), cfg)
    return _CACHE[cfg_key]


def make_in_maps(cfg, inputs):
    x = np.asarray(inputs['x'], np.float32)
    ctx = np.asarray(inputs['ctx'], np.float32)
    c = np.asarray(inputs['c'], np.float32)
    c_ctx = np.asarray(inputs['c_ctx'], np.float32)
    B = x.shape[0]
    cosT, sinT, cmat, csel, mmask = host_consts(cfg)
    shared = {k: np.ascontiguousarray(np.asarray(inputs[k], np.float32)) for k in (
        'w_ada', 'b_ada', 'w_in', 'mlstm_conv_w', 'mlstm_conv_b', 'mlstm_gate_b', 'mlstm_norm_g', 'diff_lambda',
        'diff_norm_g', 'gqa_q_norm_g', 'gqa_k_norm_g', 'w_out', 'ln1_g', 'ln1_b', 'w_ffn_in', 'w_ffn_out', 'ln2_g', 'ln2_b')}
    Lh = DEPTH_FULL
    pp = np.zeros((Lh, 128, 64), np.float32)
    partner = np.arange(64)
    partner = np.where((partner % 32) < 16, partner + 16, partner - 16)
    for l in range(Lh):
        qg = shared['gqa_q_norm_g'][l]
        kg = shared['gqa_k_norm_g'][l]
        pp[l, :, 0] = np.tile(qg, 2)
        pp[l, :, 1] = np.tile(kg, 2)
        pp[l, :, 2] = np.tile(qg[partner], 2)
        pp[l, :, 3] = np.tile(kg[partner], 2)
        pp[l, :, 4:8] = shared['mlstm_conv_b'][l].reshape(4, 128).T
        pp[l, :, 8:28] = shared['mlstm_conv_w'][l].reshape(5, 4, 128).transpose(2, 1, 0).reshape(128, 20)
        pp[l, :, 28] = shared['diff_norm_g'][l]
        pp[l, :, 29:31] = shared['mlstm_norm_g'][l].reshape(2, 128).T
        pp[l, :, 32:40] = shared['ln1_g'][l].reshape(8, 128).T
        pp[l, :, 40:48] = shared['ln1_b'][l].reshape(8, 128).T
        pp[l, :, 48:56] = shared['ln2_g'][l].reshape(8, 128).T
        pp[l, :, 56:64] = shared['ln2_b'][l].reshape(8, 128).T
    shared.update(cosT=cosT, sinT=sinT, cmat=cmat, csel=csel, pp=pp, mmask=mmask)
    maps = []
    for b in range(B):
        m = dict(shared)
        m['xT_in'] = np.ascontiguousarray(np.concatenate([ctx[b].T, x[b].T], axis=1))
        m['c2T'] = np.ascontiguousarray(np.stack([c[b], c_ctx], axis=1))
        maps.append(m)
    return maps


def kernel(**inputs):
    x = inputs['x']
    B, S, _ = x.shape
    (nc, K), cfg = get_program((S, DEPTH_FULL))
    maps = make_in_maps(cfg, inputs)
    res = run_bass_kernel_spmd(nc, maps, core_ids=list(range(B)))
    out = np.stack([np.ascontiguousarray(r['outT'].T) for r in res.results], axis=0)
    return out.astype(np.float32)
```
